# Optimizing a Trainium2 kernel written in Bass

```python
import math
import jax, jax.numpy as jnp
from jax import lax
import numpy as np

D_MODEL = 1024
BATCH = 2
SEQ = 8192
DEPTH = 1
DEC_BATCH = 32
DEC_SEQ = 1
PAST_LEN = 16384
PAGE_SIZE = 128

N_MEM = 256
SB_HEADS = 8
SB_HEAD_DIM = 64
SB_WIDTH = SB_HEADS * SB_HEAD_DIM
SB_BIAS_INIT = -6.0
CONV_WIDTH = D_MODEL // 2
CONV_K = 3
XA_HEADS = 4
XA_HEAD_DIM = D_MODEL // 8
XA_WIDTH = XA_HEADS * XA_HEAD_DIM
D_FF = 4 * D_MODEL
Q_BLOCK = 128
N_BRANCH = 3
RMS_EPS = 1e-6
IN_COLS = 3 * SB_WIDTH + 3 * CONV_WIDTH + XA_WIDTH
IN_SPLITS = tuple(np.cumsum([SB_WIDTH, SB_WIDTH, SB_WIDTH, CONV_WIDTH, CONV_WIDTH, CONV_WIDTH])[:].tolist())

kernel_name = 'stickbreak_shortconv_memxattn_hybrid_step'


def rms_norm(x, g):
    xf = x.astype(jnp.float32)
    y = xf * lax.rsqrt(jnp.mean(xf * xf, axis=-1, keepdims=True) + RMS_EPS)
    return (y * g.astype(jnp.float32)).astype(x.dtype)


def split_in(u):
    q, k, v, cb, cc, ch, xq = jnp.split(u, IN_SPLITS, axis=-1)
    b, t = u.shape[0], u.shape[1]
    hd = lambda a, h, d: a.reshape(b, t, h, d)
    return (hd(q, SB_HEADS, SB_HEAD_DIM), hd(k, SB_HEADS, SB_HEAD_DIM), hd(v, SB_HEADS, SB_HEAD_DIM),
            cb, cc, ch, hd(xq, XA_HEADS, XA_HEAD_DIM))


def sb_block(q, q_pos, k, v, k_pos, b_sb):
    z = jnp.einsum('bqhd,bkhd->bhqk', q.astype(jnp.float32), k.astype(jnp.float32)) * (1.0 / math.sqrt(SB_HEAD_DIM))
    z = z + b_sb.astype(jnp.float32)[None, :, None, None]
    mask = k_pos[None, :] < q_pos[:, None]
    log_keep = jnp.where(mask, jax.nn.log_sigmoid(-z), 0.0)
    after = lax.cumsum(log_keep, axis=3, reverse=True) - log_keep
    w = jnp.where(mask, jnp.exp(jax.nn.log_sigmoid(z) + after), 0.0)
    return jnp.einsum('bhqk,bkhd->bqhd', w, v.astype(jnp.float32)).astype(v.dtype)


def sb_prompt(q, k, v, b_sb):
    b, s, h, d = q.shape
    nb = s // Q_BLOCK
    qb = q.reshape(b, nb, Q_BLOCK, h, d).transpose(1, 0, 2, 3, 4)
    pos = jnp.arange(s)
    qp = pos.reshape(nb, Q_BLOCK)
    out = lax.map(lambda a: sb_block(a[0], a[1], k, v, pos, b_sb), (qb, qp))
    return out.transpose(1, 0, 2, 3, 4).reshape(b, s, h * d)


def causal_conv(u_ext, w):
    t = u_ext.shape[1] - (CONV_K - 1)
    return sum(u_ext[:, i:i + t] * w[i] for i in range(CONV_K))


def mem_kv(mem, g_mem, w_mem_kv):
    kv = rms_norm(mem, g_mem) @ w_mem_kv
    b, m = mem.shape[0], mem.shape[1]
    mk, mv = jnp.split(kv, 2, axis=-1)
    return mk.reshape(b, m, XA_HEADS, XA_HEAD_DIM), mv.reshape(b, m, XA_HEADS, XA_HEAD_DIM)


def cross_attn(q, mk, mv):
    s = jnp.einsum('bqhd,bmhd->bhqm', q.astype(jnp.float32), mk.astype(jnp.float32)) * (1.0 / math.sqrt(XA_HEAD_DIM))
    p = jax.nn.softmax(s, axis=-1)
    o = jnp.einsum('bhqm,bmhd->bqhd', p, mv.astype(jnp.float32)).astype(mv.dtype)
    return o.reshape(q.shape[0], q.shape[1], XA_WIDTH)


def merge_and_ffn(x, xn, y_sb, y_conv, y_xa, w_gate, b_gate, w_sb_o, w_conv_o, w_xa_o, w_o,
                  g_mix_post, g_ffn_pre, w_up, w_down, g_ffn_post):
    g = jax.nn.sigmoid((xn @ w_gate + b_gate).astype(jnp.float32)).astype(x.dtype)
    g = g.reshape(x.shape[0], x.shape[1], N_BRANCH, D_MODEL)
    m = g[:, :, 0] * (y_sb @ w_sb_o) + g[:, :, 1] * (y_conv @ w_conv_o) + g[:, :, 2] * (y_xa @ w_xa_o)
    h = x + rms_norm(m @ w_o, g_mix_post)
    f = jnp.square(jax.nn.relu(rms_norm(h, g_ffn_pre) @ w_up)) @ w_down
    return h + rms_norm(f, g_ffn_post)


def setup_inputs(seed: int = 0) -> dict:
    key = jax.random.key(seed)
    ks = jax.random.split(key, 32)
    n_pages = PAST_LEN // PAGE_SIZE
    n_phys = (DEC_BATCH * n_pages * 5) // 4
    nrm = lambda k, shp, s: jax.random.normal(k, shp, jnp.float32) * s
    gain = lambda k: 1.0 + nrm(k, (DEPTH, D_MODEL), 0.05)
    page_table = jax.random.permutation(ks[7], n_phys)[:DEC_BATCH * n_pages].reshape(DEC_BATCH, n_pages).astype(jnp.int32)
    return {
        'x_prompt': nrm(ks[0], (BATCH, SEQ, D_MODEL), 1.0),
        'x_sample': nrm(ks[1], (DEC_BATCH, DEC_SEQ, D_MODEL), 1.0),
        'mem_prompt': nrm(ks[2], (BATCH, N_MEM, D_MODEL), 1.0),
        'cache_k_pages': nrm(ks[3], (DEPTH, n_phys, PAGE_SIZE, SB_HEADS, SB_HEAD_DIM), 1.0),
        'cache_v_pages': nrm(ks[4], (DEPTH, n_phys, PAGE_SIZE, SB_HEADS, SB_HEAD_DIM), 1.0),
        'page_table': page_table,
        'cache_mem_k': nrm(ks[5], (DEPTH, DEC_BATCH, N_MEM, XA_HEADS, XA_HEAD_DIM), 1.0),
        'cache_mem_v': nrm(ks[6], (DEPTH, DEC_BATCH, N_MEM, XA_HEADS, XA_HEAD_DIM), 1.0),
        'state_conv': nrm(ks[8], (DEPTH, DEC_BATCH, CONV_K - 1, CONV_WIDTH), 1.0),
        'g_mix_pre': gain(ks[9]),
        'w_in': nrm(ks[10], (DEPTH, D_MODEL, IN_COLS), D_MODEL ** -0.5),
        'b_sb': SB_BIAS_INIT + nrm(ks[25], (DEPTH, SB_HEADS), 0.1),
        'w_conv': nrm(ks[11], (DEPTH, CONV_K, CONV_WIDTH), CONV_K ** -0.5),
        'g_mem': gain(ks[12]),
        'w_mem_kv': nrm(ks[13], (DEPTH, D_MODEL, 2 * XA_WIDTH), D_MODEL ** -0.5),
        'w_gate': nrm(ks[14], (DEPTH, D_MODEL, N_BRANCH * D_MODEL), D_MODEL ** -0.5),
        'b_gate': nrm(ks[15], (DEPTH, N_BRANCH * D_MODEL), 0.1),
        'w_sb_o': nrm(ks[16], (DEPTH, SB_WIDTH, D_MODEL), SB_WIDTH ** -0.5),
        'w_conv_o': nrm(ks[17], (DEPTH, CONV_WIDTH, D_MODEL), CONV_WIDTH ** -0.5),
        'w_xa_o': nrm(ks[18], (DEPTH, XA_WIDTH, D_MODEL), XA_WIDTH ** -0.5),
        'w_o': nrm(ks[19], (DEPTH, D_MODEL, D_MODEL), D_MODEL ** -0.5),
        'g_mix_post': gain(ks[20]),
        'g_ffn_pre': gain(ks[21]),
        'w_up': nrm(ks[22], (DEPTH, D_MODEL, D_FF), D_MODEL ** -0.5),
        'w_down': nrm(ks[23], (DEPTH, D_FF, D_MODEL), D_FF ** -0.5),
        'g_ffn_post': gain(ks[24]),
    }


def reference(x_prompt, x_sample, mem_prompt, cache_k_pages, cache_v_pages, page_table,
              cache_mem_k, cache_mem_v, state_conv, g_mix_pre, w_in, b_sb, w_conv, g_mem, w_mem_kv,
              w_gate, b_gate, w_sb_o, w_conv_o, w_xa_o, w_o, g_mix_post, g_ffn_pre, w_up, w_down,
              g_ffn_post):
    n_pages = PAST_LEN // PAGE_SIZE
    xp, xs = x_prompt, x_sample
    kp_l, vp_l, cp_l, mkp_l, mvp_l, ks_l, vs_l, cs_l = [], [], [], [], [], [], [], []
    for l in range(DEPTH):
        tail = (w_gate[l], b_gate[l], w_sb_o[l], w_conv_o[l], w_xa_o[l], w_o[l],
                g_mix_post[l], g_ffn_pre[l], w_up[l], w_down[l], g_ffn_post[l])
        xn = rms_norm(xp, g_mix_pre[l])
        q, k, v, cb, cc, ch, xq = split_in(xn @ w_in[l])
        y_sb = sb_prompt(q, k, v, b_sb[l])
        c_ext = jnp.pad(cc * ch, ((0, 0), (CONV_K - 1, 0), (0, 0)))
        y_conv = cb * causal_conv(c_ext, w_conv[l])
        mk, mv = mem_kv(mem_prompt, g_mem[l], w_mem_kv[l])
        y_xa = cross_attn(xq, mk, mv)
        kp_l.append(k); vp_l.append(v); cp_l.append(c_ext[:, -(CONV_K - 1):])
        mkp_l.append(mk); mvp_l.append(mv)
        xp = merge_and_ffn(xp, xn, y_sb, y_conv, y_xa, *tail)
        xn = rms_norm(xs, g_mix_pre[l])
        q, k, v, cb, cc, ch, xq = split_in(xn @ w_in[l])
        k_past = cache_k_pages[l][page_table].reshape(DEC_BATCH, n_pages * PAGE_SIZE, SB_HEADS, SB_HEAD_DIM)
        v_past = cache_v_pages[l][page_table].reshape(DEC_BATCH, n_pages * PAGE_SIZE, SB_HEADS, SB_HEAD_DIM)
        k_all = jnp.concatenate([k_past, k], axis=1)
        v_all = jnp.concatenate([v_past, v], axis=1)
        k_pos = jnp.arange(PAST_LEN + DEC_SEQ)
        q_pos = PAST_LEN + jnp.arange(DEC_SEQ)
        y_sb = sb_block(q, q_pos, k_all, v_all, k_pos, b_sb[l]).reshape(DEC_BATCH, DEC_SEQ, SB_WIDTH)
        c_ext = jnp.concatenate([state_conv[l].astype(cc.dtype), cc * ch], axis=1)
        y_conv = cb * causal_conv(c_ext, w_conv[l])
        y_xa = cross_attn(xq, cache_mem_k[l], cache_mem_v[l])
        ks_l.append(k); vs_l.append(v); cs_l.append(c_ext[:, -(CONV_K - 1):])
        xs = merge_and_ffn(xs, xn, y_sb, y_conv, y_xa, *tail)
    return (xp, xs, jnp.stack(kp_l), jnp.stack(vp_l), jnp.stack(cp_l), jnp.stack(mkp_l), jnp.stack(mvp_l),
            jnp.stack(ks_l), jnp.stack(vs_l), jnp.stack(cs_l))
```

```python
import contextlib
import math
import numpy as np
import ml_dtypes
import concourse.bass as bass
import concourse.mybir as mybir
from concourse.bass_utils import run_bass_kernel_spmd

F32 = mybir.dt.float32
BF16 = mybir.dt.bfloat16
I32 = mybir.dt.int32
AF = mybir.ActivationFunctionType
ALU = mybir.AluOpType
AX = mybir.AxisListType

D = 1024
NH = 8
DH = 64
SBW = 512
CW = 512
XAH = 4
XAD = 128
NMEM = 256
DFF = 4096
INC = 3584
EPS = 1e-6
NEG = -30000.0


class Buf:
    def __init__(self, name):
        self.name = name
        self.w = None
        self.r = {}


class Tracker:
    def __init__(self, nc, es):
        self.nc = nc
        self.es = es
        self.eng = {"pe": nc.tensor, "act": nc.scalar, "dve": nc.vector, "pool": nc.gpsimd, "sp": nc.sync}
        self.sem = {k: es.enter_context(nc.semaphore("sem_" + k)) for k in ["pe", "act", "dve", "pool"]}
        self.cnt = dict.fromkeys(self.sem, 0)
        self.waited = {}
        self.dsem = {}
        self.dcnt = {}
        self.nd = 0

    def _wait(self, waiter, ev):
        kind, key, val = ev
        if kind == "eng" and key == "pe" and waiter == "pe":
            return
        k = (waiter, kind, key)
        if val <= self.waited.get(k, 0):
            return
        sem = self.sem[key] if kind == "eng" else self.dsem[key]
        self.eng[waiter].wait_ge(sem, val)
        self.waited[k] = val

    def deps(self, e, reads, writes):
        for b in reads:
            if b.w is not None:
                self._wait(e, b.w)
        for b in writes:
            if b.w is not None:
                self._wait(e, b.w)
            for ev in b.r.values():
                self._wait(e, ev)

    def op(self, e, fn, reads=(), writes=()):
        self.deps(e, reads, writes)
        ins = fn()
        ins.then_inc(self.sem[e], 1)
        self.cnt[e] += 1
        ev = ("eng", e, self.cnt[e])
        for b in reads:
            b.r[e] = ev
        for b in writes:
            b.w = ev
            b.r = {}
        return ins

    def dma(self, q, fn, reads=(), writes=(), key=None):
        self.deps(q, reads, writes)
        if key is None:
            key = (writes[0].name if writes else reads[0].name)
        if key not in self.dsem:
            self.dsem[key] = self.es.enter_context(self.nc.semaphore("dsem%d" % self.nd))
            self.nd += 1
            self.dcnt[key] = 0
        ins = fn()
        ins.then_inc(self.dsem[key], 16)
        self.dcnt[key] += 16
        ev = ("dma", key, self.dcnt[key])
        for b in reads:
            b.r["dma:" + str(key)] = ev
        for b in writes:
            b.w = ev
            b.r = {}
        return ins

    def finish(self, bufs):
        for b in bufs:
            if b.w is not None:
                self._wait("sp", b.w)
            for ev in b.r.values():
                self._wait("sp", ev)


class Ctx:
    def __init__(self, nc, es):
        self.nc = nc
        self.es = es
        self.n = 0

    def sb(self, shape, dt, name=None):
        self.n += 1
        return self.es.enter_context(self.nc.sbuf_tensor(name or ("sb%d" % self.n), list(shape), dt))

    def ps(self, shape, dt, name=None):
        self.n += 1
        return self.es.enter_context(self.nc.psum_tensor(name or ("ps%d" % self.n), list(shape), dt))


def own_groups(i, G):
    ns = G // 4
    return [4 * k + (i if k % 2 == 0 else 3 - i) for k in range(ns)]


def make_maskwin(i):
    out = np.zeros((2, 128, 16, 512), np.float32)
    p = np.arange(128)[:, None]
    c = np.arange(512)[None, :]
    for w, delta in enumerate([i, 3 - i]):
        for j in range(16):
            r = j - 4 * delta
            if r < 0:
                continue
            vis = (128 * r + p) < c
            out[w, :, j, :] = np.where(vis, 0.0, NEG)
    return out.reshape(2, 128, 16 * 512).astype(ml_dtypes.bfloat16)


DBG = False


def build_main(G):
    SEQ = 512 * G
    NS = G // 4
    NB = SEQ // 128
    TOWN = NS * 512
    nc = bass.Bass("TRN2", target_bir_lowering=False)
    din = lambda n, s, d=F32: nc.dram_tensor(n, list(s), d, kind="ExternalInput").ap()
    dout = lambda n, s, d=F32: nc.dram_tensor(n, list(s), d, kind="ExternalOutput").ap()
    dscr = lambda n, s, d=BF16: nc.dram_tensor(n, list(s), d, kind="Internal").ap()
    xb = din("xb", [SEQ, D]); xo = din("xo", [TOWN, D]); xh = din("xh", [8, D])
    maskwin = din("maskwin", [2, 128, 8192], BF16)
    ident_d = din("ident", [128, 128], BF16); tri_d = din("tri", [128, 128], BF16); fix_d = din("fixm", [128, 128], BF16)
    identf_d = din("identf", [128, 128], F32)
    mem = din("mem", [NMEM, D])
    g_preT = din("g_mix_preT", [128, 8]); w_in = din("w_in", [D, INC]); b_sb = din("b_sb", [1, NH])
    wconvT = din("wconvT", [128, 4, 3]); g_memT = din("g_memT", [128, 8]); w_mem = din("w_mem_kv", [D, 1024])
    w_gate = din("w_gate", [D, 3 * D]); b_gateT = din("b_gateT", [128, 24])
    w_sbo = din("w_sb_o", [SBW, D]); w_cvo = din("w_conv_o", [CW, D]); w_xao = din("w_xa_o", [512, D])
    w_o = din("w_o", [D, D]); g_post = din("g_mix_post", [1, D]); g_fpreT = din("g_ffn_preT", [128, 8])
    w_up = din("w_up", [D, DFF]); w_down = din("w_down", [DFF, D]); g_fpost = din("g_ffn_post", [1, D])
    xs4 = din("xs4", [4, D]); ysbT_s = din("ysbT_s", [128, 4, 4]); stT_d = din("stT", [128, 4, 2, 4])
    cmk = din("cmk", [4, NMEM, 512]); cmv = din("cmv", [4, NMEM, 512])
    ys_o = dout("ys_o", [4, D]); ks_o = dout("ks_o", [4, SBW]); vs_o = dout("vs_o", [4, SBW]); cs_o = dout("cs_o", [128, 4, 2, 4])
    y_o = dout("y_o", [TOWN, D]); k_o = dout("k_o", [TOWN, SBW]); v_o = dout("v_o", [TOWN, SBW])
    conv_o = dout("conv_o", [NS, 128, 4, 2]); mk_o = dout("mk_o", [NMEM, 512]); mv_o = dout("mv_o", [NMEM, 512])
    KTs = dscr("KTs", [4, 128, SEQ]); Vs = dscr("Vs", [4, 128, NB, 128])
    YTs = dscr("YTs", [128, 4, TOWN])

    es = contextlib.ExitStack()
    with es:
        tr = Tracker(nc, es)
        cx = Ctx(nc, es)
        out_bufs = []

        def barrier():
            for e in ["pe", "act", "dve", "pool"]:
                for s in ["pe", "act", "dve", "pool"]:
                    if s != e and tr.cnt[s] > 0:
                        tr._wait(e, ("eng", s, tr.cnt[s]))
            for k, v in tr.dcnt.items():
                for e in ["pe", "act", "dve", "pool", "sp"]:
                    tr._wait(e, ("dma", k, v))

        ident = cx.sb([128, 128], BF16); tri = cx.sb([128, 128], BF16); fixm = cx.sb([128, 128], BF16)
        identf = cx.sb([128, 128], F32)
        gpreT = cx.sb([128, 8], F32); gfpreT = cx.sb([128, 8], F32); gmemT = cx.sb([128, 8], F32)
        gpost_b = cx.sb([128, D], F32); gfpost_b = cx.sb([128, D], F32)
        bsb_b = cx.sb([128, NH], F32); wcv = cx.sb([128, 4, 3], F32); bgt = cx.sb([128, 24], F32)
        B_const = Buf("const")
        ld = lambda o, i: tr.dma("sp", lambda: nc.sync.dma_start(out=o, in_=i), writes=[B_const], key="const")
        ld(ident[:], ident_d); ld(tri[:], tri_d); ld(fixm[:], fix_d); ld(identf[:], identf_d)
        ld(gpreT[:], g_preT); ld(gfpreT[:], g_fpreT); ld(gmemT[:], g_memT)
        ld(gpost_b[:], g_post.partition_broadcast(128)); ld(gfpost_b[:], g_fpost.partition_broadcast(128))
        ld(bsb_b[:], b_sb.partition_broadcast(128))
        ld(wcv[:], wconvT); ld(bgt[:], b_gateT)

        bigs = [cx.ps([128, 1024], F32, name="big%d" % i) for i in range(4)]
        banks = [bigs[i // 2][:, (i % 2) * 512:(i % 2 + 1) * 512] for i in range(8)]
        Bbank = [Buf("bank%d" % i) for i in range(8)]

        NXT = 3
        xt = [cx.sb([128, D], F32) for _ in range(NXT)]; B_xt = [Buf("xt%d" % i) for i in range(NXT)]
        ss = cx.sb([128, 8], F32); B_ss = [Buf("ss%d" % i) for i in range(4)]
        xn = [cx.sb([128, D], BF16) for _ in range(2)]; B_xn = [Buf("xn0"), Buf("xn1")]
        xnT = [cx.sb([128, 8, 512], BF16) for _ in range(2)]; B_xnT = [Buf("xnT0"), Buf("xnT1")]
        state = {"tile": 0, "bank": 0, "ev": 0, "ft": 0}

        def nextbank():
            b = state["bank"] % 8
            state["bank"] += 1
            return banks[b], Bbank[b]

        def evac(out_ap, in_ap, reads, writes):
            state["ev"] += 1
            if state["ev"] % 2 == 0:
                tr.op("act", lambda: nc.scalar.copy(out=out_ap, in_=in_ap), reads=reads, writes=writes)
            else:
                tr.op("dve", lambda: nc.vector.tensor_copy(out=out_ap, in_=in_ap), reads=reads, writes=writes)

        def rstd_of(X, BX, nrows, i):
            BS = B_ss[i % 4]
            sc = ss[:, 2 * (i % 4):2 * (i % 4) + 1]; sc2 = ss[:, 2 * (i % 4) + 1:2 * (i % 4) + 2]
            XN = xn[i % 2]; BXN = B_xn[i % 2]
            tr.op("act", lambda: nc.scalar.activation(out=XN[0:nrows, :], in_=X, func=AF.Square,
                                                      accum_out=sc[0:nrows, :]), reads=(BX if isinstance(BX, list) else [BX]), writes=[BXN, BS])
            tr.op("act", lambda: nc.scalar.activation(out=sc2[0:nrows, :], in_=sc[0:nrows, :], func=AF.Ln,
                                                      bias=float(EPS), scale=1.0 / D), reads=[BS], writes=[BS])
            tr.op("act", lambda: nc.scalar.activation(out=sc[0:nrows, :], in_=sc2[0:nrows, :], func=AF.Exp, scale=-0.5),
                  reads=[BS], writes=[BS])
            return sc[0:nrows, :], BS

        def norm_T(X, BX, nrows, xT, B_xT, col0, gT):
            i = state["tile"]; state["tile"] += 1
            XN = xn[i % 2]; BXN = B_xn[i % 2]
            sc, BS = rstd_of(X, BX, nrows, i)
            tr.op("dve", lambda: nc.vector.tensor_scalar(out=XN[0:nrows, :], in0=X, scalar1=sc, scalar2=None, op0=ALU.mult),
                  reads=(BX if isinstance(BX, list) else [BX]) + [BS], writes=[BXN])
            bk, Bbk = nextbank()
            bkb = bk[:].bitcast(BF16)
            for c in range(8):
                tr.op("pe", lambda c=c: nc.tensor.transpose(out=bkb[:, c * 128:c * 128 + nrows],
                                                            in_=XN[0:nrows, c * 128:(c + 1) * 128], identity=ident[0:nrows, 0:nrows]),
                      reads=[BXN, B_const], writes=[Bbk])
            tr.op("dve", lambda: nc.vector.tensor_tensor(
                out=xT[:, :, col0:col0 + nrows], in0=bkb.rearrange("p (c t) -> p c t", t=128)[:, :, 0:nrows],
                in1=gT[:, :].unsqueeze(2).to_broadcast([128, 8, nrows]), op=ALU.mult), reads=[Bbk, B_const], writes=[B_xT])

        def front_tile(src_rows, nrows, xT, B_xT, col0, gT):
            j = state["ft"] % NXT; state["ft"] += 1
            tr.dma("sp", lambda: nc.sync.dma_start(out=xt[j][0:nrows, :], in_=src_rows), writes=[B_xt[j]])
            norm_T(xt[j][0:nrows, :], B_xt[j], nrows, xT, B_xT, col0, gT)

        def front_group(src, xT, B_xT, gT):
            for t in range(4):
                front_tile(src[t * 128:(t + 1) * 128, :], 128, xT, B_xT, t * 128, gT)

        es_qa = contextlib.ExitStack()
        es_qa.__enter__()
        cqa = Ctx(nc, es_qa); cqa.n = 500
        QT = cqa.sb([128, 4, TOWN], BF16); B_QT = Buf("QT")
        YT = cqa.sb([128, 4, TOWN], BF16); B_YT = [Buf("YT%d" % k) for k in range(NS)]

        wv = lambda w: w.rearrange("(kc p) n -> p kc n", p=128)
        chunks = []
        for j in range(4):
            chunks.append(wv(w_in)[:, :, 1536 + 512 * j:1536 + 512 * (j + 1)])
        for half in range(2):
            for br in range(3):
                chunks.append(wv(w_gate)[:, :, br * D + half * 512: br * D + (half + 1) * 512])
        for half in range(2):
            for w in (w_sbo, w_cvo, w_xao):
                chunks.append(wv(w)[:, :, half * 512:(half + 1) * 512])
        for half in range(2):
            chunks.append(wv(w_o)[:, :, half * 512:(half + 1) * 512])
        for j in range(8):
            chunks.append(wv(w_up)[:, :, 512 * j:512 * (j + 1)])
        for half in range(2):
            for kg in range(4):
                chunks.append(wv(w_down)[:, 8 * kg:8 * kg + 8, half * 512:(half + 1) * 512])
        chunks.append(wv(w_in)[:, :, 512:1024]); chunks.append(wv(w_in)[:, :, 1024:1536])
        assert len(chunks) == 36
        wsc2 = dscr("wsc2", [36, 128, 4096])

        with contextlib.ExitStack() as es1:
            c1 = Ctx(nc, es1)
            c1.n = 1000
            wqkv = c1.sb([128, 8, 1536], BF16); B_wqkv = Buf("wqkv")
            for j in range(3):
                tr.dma("pool", lambda j=j: nc.gpsimd.dma_start(out=wqkv[:, :, 512 * j:512 * (j + 1)],
                                                               in_=wv(w_in)[:, :, 512 * j:512 * (j + 1)]),
                       writes=[B_wqkv], key="wqkv%d" % j)
            KTst = [c1.sb([128, 4, 512], BF16) for _ in range(2)]; B_KTst = [Buf("KTst0"), Buf("KTst1")]
            Vst = [c1.sb([128, 4, 512], BF16) for _ in range(2)]; B_Vst = [Buf("Vst0"), Buf("Vst1")]
            kvst = [c1.sb([128, 512], F32) for _ in range(2)]; B_kvst = [Buf("kvst0"), Buf("kvst1")]

            items = [("b", g) for g in range(G)] + [("o", k) for k in range(NS)]

            def src_of(it):
                kind, j = it
                return xb[j * 512:(j + 1) * 512, :] if kind == "b" else xo[j * 512:(j + 1) * 512, :]

            def item_mm(it, t, gb):
                kind, j = it
                if kind == "b":
                    p = t
                    bk, Bbk = nextbank()
                    for c in range(8):
                        tr.op("pe", lambda c=c, bk=bk: nc.tensor.matmul(bk[:], lhsT=wqkv[:, c, 512 + p * 128:512 + (p + 1) * 128],
                                                                        rhs=xnT[gb][:, c, :], start=(c == 0), stop=(c == 7)),
                              reads=[B_wqkv, B_xnT[gb]], writes=[Bbk])
                    evac(KTst[gb][:, p, :], bk[:], [Bbk], [B_KTst[gb]])
                    bk, Bbk = nextbank()
                    for c in range(8):
                        tr.op("pe", lambda c=c, bk=bk: nc.tensor.matmul(bk[:], lhsT=xnT[gb][:, c, t * 128:(t + 1) * 128],
                                                                        rhs=wqkv[:, c, 1024:1536], start=(c == 0), stop=(c == 7)),
                              reads=[B_wqkv, B_xnT[gb]], writes=[Bbk])
                    evac(Vst[gb][:, t, :], bk[:], [Bbk], [B_Vst[gb]])
                else:
                    k = j; p = t
                    bk, Bbk = nextbank()
                    for c in range(8):
                        tr.op("pe", lambda c=c, bk=bk: nc.tensor.matmul(bk[:], lhsT=wqkv[:, c, p * 128:(p + 1) * 128],
                                                                        rhs=xnT[gb][:, c, :], start=(c == 0), stop=(c == 7)),
                              reads=[B_wqkv, B_xnT[gb]], writes=[Bbk])
                    evac(QT[:, p, k * 512:(k + 1) * 512], bk[:], [Bbk], [B_QT])
                    for which, dst in ((0, k_o), (1, v_o)):
                        bk, Bbk = nextbank()
                        for c in range(8):
                            tr.op("pe", lambda c=c, bk=bk, which=which: nc.tensor.matmul(
                                bk[:], lhsT=xnT[gb][:, c, t * 128:(t + 1) * 128],
                                rhs=wqkv[:, c, 512 + 512 * which:1024 + 512 * which], start=(c == 0), stop=(c == 7)),
                                reads=[B_wqkv, B_xnT[gb]], writes=[Bbk])
                        sb_i = (2 * t + which) % 2
                        evac(kvst[sb_i][:], bk[:], [Bbk], [B_kvst[sb_i]])
                        B_o = Buf("kvo"); out_bufs.append(B_o)
                        r0 = k * 512 + t * 128
                        tr.dma("pool", lambda dst=dst, r0=r0, sb_i=sb_i: nc.gpsimd.dma_start(out=dst[r0:r0 + 128, :], in_=kvst[sb_i][:]),
                               reads=[B_kvst[sb_i]], writes=[B_o], key="kvst%d" % sb_i)

            def item_store(it, gb):
                kind, g = it
                if kind != "b":
                    return
                tr.dma("pool", lambda: nc.gpsimd.dma_start(out=KTs[:, :, g * 512:(g + 1) * 512].rearrange("q p n -> p q n"),
                                                           in_=KTst[gb][:]), reads=[B_KTst[gb]], key="kts%d" % gb)
                for q in range(4):
                    tr.dma("pool", lambda q=q: nc.gpsimd.dma_start(
                        out=Vs[q, :, 4 * g:4 * g + 4, :], in_=Vst[gb][:, :, q * 128:(q + 1) * 128]),
                        reads=[B_Vst[gb]], key="vs%d_%d" % (gb, q))

            front_group(src_of(items[0]), xnT[0], B_xnT[0], gpreT)
            for idx, it in enumerate(items):
                gb = idx % 2
                nxt = items[idx + 1] if idx + 1 < len(items) else None
                for t in range(4):
                    if nxt is not None:
                        front_tile(src_of(nxt)[t * 128:(t + 1) * 128, :], 128, xnT[1 - gb], B_xnT[1 - gb], t * 128, gpreT)
                    item_mm(it, t, gb)
                item_store(it, gb)
            barrier()
        with contextlib.ExitStack() as es2:
            c2 = Ctx(nc, es2)
            c2.n = 2000
            mw = c2.sb([128, 2, 8192], BF16); B_mw = Buf("mw")
            wtmp = c2.sb([128, 4096], BF16); B_wtmp = Buf("wtmp")
            for j, ch in enumerate(chunks):
                kc = ch.shape[1]
                tr.dma("pool", lambda ch=ch, kc=kc: nc.gpsimd.dma_start(
                    out=wtmp[:, 0:kc * 512].rearrange("p (k n) -> p k n", n=512), in_=ch), writes=[B_wtmp], key="wtl")
                tr.dma("pool", lambda j=j: nc.gpsimd.dma_start(out=wsc2[j], in_=wtmp[:]), reads=[B_wtmp], key="wts")
            tr.dma("sp", lambda: nc.sync.dma_start(out=mw[:], in_=maskwin.rearrange("w p n -> p w n")), writes=[B_mw])
            KT = [c2.sb([128, SEQ], BF16) for _ in range(2)]; B_KT = [Buf("KT0"), Buf("KT1")]
            VV = [c2.sb([128, NB, 128], BF16) for _ in range(2)]; B_VV = [Buf("VV0"), Buf("VV1")]
            Zb = [[banks[0], banks[1]], [banks[2], banks[3]]]; B_Zb = [[Bbank[0], Bbank[1]], [Bbank[2], Bbank[3]]]
            NE, NL, NG, NW = 4, 3, 2, 2
            Eb = [c2.sb([128, 2, 512], BF16) for _ in range(NE)]; B_E = [[Buf("E%d_%d" % (i, h)) for h in range(2)] for i in range(NE)]
            Lb = [c2.sb([128, 2, 512], BF16) for _ in range(NL)]; B_L = [Buf("L%d" % i) for i in range(NL)]
            Gb = [c2.sb([128, 2, 512], BF16) for _ in range(NG)]; B_G = [Buf("G%d" % i) for i in range(NG)]
            Wb = [c2.sb([128, 2, 512], BF16) for _ in range(NW)]; B_W = [Buf("W%d" % i) for i in range(NW)]
            for p in range(4):
                tr.dma("sp", lambda p=p: nc.sync.dma_start(out=KT[p % 2][:], in_=KTs[p]), writes=[B_KT[p % 2]])
                tr.dma("sp", lambda p=p: nc.sync.dma_start(out=VV[p % 2][:], in_=Vs[p]), writes=[B_VV[p % 2]])
                steps = [(k, kb) for k in range(NS) for kb in range(16 * (k + 1) - 1, -1, -1)]
                NSTEP = len(steps)
                KTp = KT[p % 2]; VVp = VV[p % 2]; BKT = B_KT[p % 2]; BVV = B_VV[p % 2]

                def emit_Z(n):
                    k, kb = steps[n]
                    masked = kb >= 16 * k
                    for h in range(2):
                        ph = slice(64 * h, 64 * h + 64)
                        zb = Zb[n % 2][h]
                        tr.op("pe", lambda zb=zb, ph=ph: nc.tensor.matmul(zb[:], lhsT=KTp[ph, kb * 128:(kb + 1) * 128],
                                                                          rhs=QT[ph, p, k * 512:(k + 1) * 512], start=True, stop=True),
                              reads=[BKT, B_QT], writes=[B_Zb[n % 2][h]])
                    if masked:
                        j = kb - 16 * k
                        for h in range(2):
                            zb = Zb[n % 2][h]
                            tr.op("pe", lambda zb=zb, j=j: nc.tensor.matmul(zb[:], lhsT=ident[:, :], rhs=mw[:, k % 2, j * 512:(j + 1) * 512],
                                                                            start=False, stop=True),
                                  reads=[B_mw, B_const], writes=[B_Zb[n % 2][h]])

                def emit_E(n):
                    for h in range(2):
                        hd = 2 * p + h
                        tr.op("act", lambda h=h, hd=hd: nc.scalar.activation(out=Eb[n % NE][:, h, :], in_=Zb[n % 2][h][:], func=AF.Exp,
                                                                             bias=bsb_b[:, hd:hd + 1], scale=0.125),
                              reads=[B_Zb[n % 2][h], B_const], writes=[B_E[n % NE][h]])

                def emit_L(n):
                    tr.op("act", lambda: nc.scalar.activation(out=Lb[n % NL][:], in_=Eb[n % NE][:], func=AF.Ln, bias=1.0, scale=1.0),
                          reads=B_E[n % NE], writes=[B_L[n % NL]])

                def emit_Tri(n):
                    k, kb = steps[n]
                    first = kb == 16 * (k + 1) - 1
                    for h in range(2):
                        tr.op("pe", lambda h=h: nc.tensor.matmul(banks[4 + h][:], lhsT=tri[:, :], rhs=Lb[n % NL][:, h, :], start=first, stop=True),
                              reads=[B_L[n % NL], B_const], writes=[Bbank[4 + h]])

                def emit_G(n):
                    tr.op("act", lambda: nc.scalar.activation(out=Gb[n % NG][:].rearrange("p h n -> p (h n)"), in_=bigs[2][:], func=AF.Exp, scale=-1.0),
                          reads=[Bbank[4], Bbank[5]], writes=[B_G[n % NG]])

                def emit_fix(n):
                    k, kb = steps[n]
                    if kb == 0:
                        return
                    for h in range(2):
                        tr.op("pe", lambda h=h: nc.tensor.matmul(banks[4 + h][:], lhsT=fixm[:, :], rhs=Lb[n % NL][:, h, :], start=False, stop=True),
                              reads=[B_L[n % NL], B_const], writes=[Bbank[4 + h]])

                def emit_W(n):
                    tr.op("dve", lambda: nc.vector.tensor_tensor(out=Wb[n % NW][:], in0=Eb[n % NE][:], in1=Gb[n % NG][:], op=ALU.mult),
                          reads=B_E[n % NE] + [B_G[n % NG]], writes=[B_W[n % NW]])

                def emit_V(n):
                    k, kb = steps[n]
                    first = kb == 16 * (k + 1) - 1
                    yb = 6 + (k % 2)
                    for h in range(2):
                        tr.op("pe", lambda h=h: nc.tensor.matmul(banks[yb][64 * h:64 * h + 64, :], lhsT=VVp[:, kb, 64 * h:64 * h + 64],
                                                                 rhs=Wb[n % NW][:, h, :], start=first, stop=(kb == 0)),
                              reads=[B_W[n % NW], BVV], writes=[Bbank[yb]])
                    if kb == 0:
                        tr.op("dve", lambda: nc.vector.tensor_copy(out=YT[:, p, k * 512:(k + 1) * 512], in_=banks[yb][:]),
                              reads=[Bbank[yb]], writes=[B_YT[k]])

                emit_Z(0); emit_E(0); emit_L(0)
                if NSTEP > 1:
                    emit_Z(1); emit_E(1); emit_L(1)
                for n in range(NSTEP):
                    emit_Tri(n)
                    if n + 2 < NSTEP:
                        emit_Z(n + 2)
                    emit_G(n)
                    if n + 2 < NSTEP:
                        emit_E(n + 2)
                        emit_L(n + 2)
                    if n >= 1:
                        emit_V(n - 1)
                    emit_fix(n)
                    emit_W(n)
                emit_V(NSTEP - 1)
            barrier()
        B_YTs = Buf("YTs")
        tr.dma("sp", lambda: nc.sync.dma_start(out=YTs, in_=YT[:]), reads=B_YT, writes=[B_YTs])
        if DBG:
            ytd = dout("yt_dbg", [128, 4, TOWN], BF16)
            B_o = Buf("ytd"); out_bufs.append(B_o)
            tr.dma("sp", lambda: nc.sync.dma_start(out=ytd, in_=YT[:]), reads=B_YT, writes=[B_o])
        barrier()
        es_qa.__exit__(None, None, None)
        with contextlib.ExitStack() as es3:
            c3 = Ctx(nc, es3); c3.n = 3000
            NR = 4
            ring = [c3.sb([128, 4096], BF16) for _ in range(NR)]; B_ring = [Buf("ring%d" % i) for i in range(NR)]
            r3 = lambda t: t[:].rearrange("p (k n) -> p k n", n=512)

            def slot_order():
                o = [1, 2, 0, 3]
                for half in range(2):
                    for br in range(3):
                        o += [4 + half * 3 + br, 10 + half * 3 + br]
                o += [16, 17] + list(range(18, 26)) + list(range(26, 34))
                return o
            use_list = []
            for k in range(NS):
                use_list += slot_order()
            use_list += [34, 35] + slot_order()
            wstate = {"issued": 0, "used": 0}

            def issue_to(n):
                while wstate["issued"] < min(n, len(use_list)):
                    i = wstate["issued"]; j = use_list[i]; sl = i % NR
                    tr.dma("pool", lambda j=j, sl=sl: nc.gpsimd.dma_start(out=ring[sl][:], in_=wsc2[j]),
                           writes=[B_ring[sl]], key="ring%d" % sl)
                    wstate["issued"] += 1

            def next_chunk(expect):
                i = wstate["used"]
                assert use_list[i] == expect, (i, use_list[i], expect)
                issue_to(i + NR - 1)
                wstate["used"] += 1
                return r3(ring[i % NR]), B_ring[i % NR]

            xres = c3.sb([128, 4, D], F32); B_xres = Buf("xres")
            ysbT = c3.sb([128, 4, 512], BF16); B_ysbT = Buf("ysbT")
            ccT = c3.sb([128, 4, 512], F32); B_ccT = Buf("ccT")
            cbT = c3.sb([128, 4, 512], BF16); B_cbT = Buf("cbT")
            pext = c3.sb([128, 4, 514], F32); B_pext = Buf("pext")
            xqT = c3.sb([128, 4, 512], BF16); B_xqT = Buf("xqT")
            yconvT = c3.sb([128, 4, 512], BF16); B_ycv = Buf("ycv")
            yxaT = c3.sb([128, 4, 512], BF16); B_yxa = Buf("yxa")
            mT = c3.sb([128, 8, 512], BF16); B_mT = Buf("mT")
            fsb = c3.sb([128, 4, 512], F32); B_fsb = Buf("fsb")
            f1T = c3.sb([128, 32, 512], BF16); B_f1T = Buf("f1T")
            gsb = c3.sb([128, 512], F32); B_gsb = Buf("gsb")
            acc = c3.sb([128, 512], F32); B_acc = Buf("acc")
            sqb = c3.sb([128, 512], F32); B_sqb = Buf("sqb")
            macc = c3.sb([128, 4, 512], F32); B_macc = Buf("macc")
            mkT = c3.sb([128, 4, 256], BF16); mvb = c3.sb([128, 2, 512], BF16); B_mkv = Buf("mkv")
            Pf = c3.sb([128, 4, 256], F32); B_Pf = Buf("Pf")
            Pn = c3.sb([128, 4, 256], BF16); B_Pn = Buf("Pn")
            PT = c3.sb([128, 8, 128], BF16); B_PT = Buf("PT")
            xnTh = c3.sb([128, 8, 8], BF16); B_xnTh = Buf("xnTh")
            ccH = c3.sb([128, 4, 8], F32); pH = c3.sb([128, 4, 8], F32); B_H = Buf("halo")
            smx = c3.sb([128, 16], F32); B_smx = Buf("smx")
            XS = 1.0 / math.sqrt(XAD)

            def mm8(bk, Bbk, lhs_fn, rhs_fn, reads, n=8):
                for c in range(n):
                    tr.op("pe", lambda c=c: nc.tensor.matmul(bk, lhsT=lhs_fn(c), rhs=rhs_fn(c), start=(c == 0), stop=(c == n - 1)),
                          reads=reads, writes=[Bbk])

            def pairbank():
                if state["bank"] % 2:
                    state["bank"] += 1
                b = state["bank"] % 8
                state["bank"] += 2
                return bigs[b // 2][:], [Bbank[b], Bbank[b + 1]]

            tr.dma("pool", lambda: nc.gpsimd.dma_start(out=r3(ring[0]), in_=wv(w_mem)[:, :, 0:512]), writes=[B_ring[0]], key="ring0")
            tr.dma("pool", lambda: nc.gpsimd.dma_start(out=r3(ring[1]), in_=wv(w_mem)[:, :, 512:1024]), writes=[B_ring[1]], key="ring1")
            wmk = r3(ring[0]); wmv = r3(ring[1])
            memT = xnT[0]
            for mt in range(2):
                front_tile(mem[mt * 128:(mt + 1) * 128, :], 128, memT, B_xnT[0], mt * 128, gmemT)
            for h in range(4):
                bk, Bbk = nextbank()
                mm8(bk[:, 0:256], Bbk, lambda c, h=h: wmk[:, c, h * 128:(h + 1) * 128], lambda c: memT[:, c, 0:256], [B_ring[0], B_xnT[0]])
                evac(mkT[:, h, :], bk[:, 0:256], [Bbk], [B_mkv])
            for mt in range(2):
                for which, wsrc, dst in ((0, wmk, mk_o), (1, wmv, mv_o)):
                    bk, Bbk = nextbank()
                    mm8(bk, Bbk, lambda c, mt=mt: memT[:, c, mt * 128:(mt + 1) * 128], lambda c, wsrc=wsrc: wsrc[:, c, :],
                        [B_ring[which], B_xnT[0]])
                    evac(fsb[:, 2 * mt + which, :], bk, [Bbk], [B_fsb])
                    if which == 1:
                        evac(mvb[:, mt, :], bk, [Bbk], [B_mkv])
                    B_o = Buf("mo"); out_bufs.append(B_o)
                    tr.dma("sp", lambda dst=dst, mt=mt, which=which: nc.sync.dma_start(out=dst[mt * 128:(mt + 1) * 128, :],
                                                                                      in_=fsb[:, 2 * mt + which, :]),
                           reads=[B_fsb], writes=[B_o], key="mo%d%d" % (mt, which))
            front_tile(xh[0:8, :], 8, xnTh, B_xnTh, 0, gpreT)

            for k in range(NS):
                X0 = xnT[0]; BX0 = B_xnT[0]; HN = xnT[1]; BHN = B_xnT[1]
                front_group(xo[k * 512:(k + 1) * 512, :], X0, BX0, gpreT)
                tr.dma("sp", lambda k=k: nc.sync.dma_start(out=xres[:], in_=xo[k * 512:(k + 1) * 512, :].rearrange("(t p) d -> p t d", p=128)),
                       writes=[B_xres])
                tr.dma("sp", lambda k=k: nc.sync.dma_start(out=ysbT[:], in_=YTs[:, :, k * 512:(k + 1) * 512]), reads=[B_YTs], writes=[B_ysbT])
                ch3, Bch = next_chunk(1)
                for oc in range(4):
                    bk, Bbk = nextbank()
                    mm8(bk, Bbk, lambda c, oc=oc: ch3[:, c, oc * 128:(oc + 1) * 128], lambda c: X0[:, c, :], [Bch, BX0])
                    evac(ccT[:, oc, :], bk, [Bbk], [B_ccT])
                    if k == 0:
                        bk, Bbk = nextbank()
                        mm8(bk[:, 0:8], Bbk, lambda c, oc=oc: ch3[:, c, oc * 128:(oc + 1) * 128], lambda c: xnTh[:, c, :], [Bch, B_xnTh])
                        evac(ccH[:, oc, :], bk[:, 0:8], [Bbk], [B_H])
                ch3, Bch = next_chunk(2)
                for oc in range(4):
                    bk, Bbk = nextbank()
                    mm8(bk, Bbk, lambda c, oc=oc: ch3[:, c, oc * 128:(oc + 1) * 128], lambda c: X0[:, c, :], [Bch, BX0])
                    tr.op("dve", lambda oc=oc, bk=bk: nc.vector.tensor_tensor(out=pext[:, oc, 2:514], in0=ccT[:, oc, :], in1=bk, op=ALU.mult),
                          reads=[Bbk, B_ccT], writes=[B_pext])
                    if k == 0:
                        bk, Bbk = nextbank()
                        mm8(bk[:, 0:8], Bbk, lambda c, oc=oc: ch3[:, c, oc * 128:(oc + 1) * 128], lambda c: xnTh[:, c, :], [Bch, B_xnTh])
                        tr.op("dve", lambda oc=oc, bk=bk: nc.vector.tensor_tensor(out=pH[:, oc, :], in0=ccH[:, oc, :], in1=bk[:, 0:8], op=ALU.mult),
                              reads=[Bbk, B_H], writes=[B_H])
                tr.op("dve", lambda k=k: nc.vector.tensor_copy(out=pext[:, :, 0:2], in_=pH[:, :, 2 * k:2 * k + 2]), reads=[B_H], writes=[B_pext])
                ch3, Bch = next_chunk(0)
                for oc in range(4):
                    bk, Bbk = nextbank()
                    mm8(bk, Bbk, lambda c, oc=oc: ch3[:, c, oc * 128:(oc + 1) * 128], lambda c: X0[:, c, :], [Bch, BX0])
                    evac(cbT[:, oc, :], bk, [Bbk], [B_cbT])
                ch3, Bch = next_chunk(3)
                for oc in range(4):
                    bk, Bbk = nextbank()
                    mm8(bk, Bbk, lambda c, oc=oc: ch3[:, c, oc * 128:(oc + 1) * 128], lambda c: X0[:, c, :], [Bch, BX0])
                    evac(xqT[:, oc, :], bk, [Bbk], [B_xqT])
                for oc in range(4):
                    tr.op("dve", lambda oc=oc: nc.vector.tensor_scalar(out=acc[:], in0=pext[:, oc, 0:512], scalar1=wcv[:, oc, 0:1], scalar2=None,
                                                                       op0=ALU.mult), reads=[B_pext, B_const], writes=[B_acc])
                    for i in (1, 2):
                        tr.op("dve", lambda oc=oc, i=i: nc.vector.scalar_tensor_tensor(out=acc[:], in0=pext[:, oc, i:i + 512], scalar=wcv[:, oc, i:i + 1],
                                                                                       in1=acc[:], op0=ALU.mult, op1=ALU.add),
                              reads=[B_pext, B_const, B_acc], writes=[B_acc])
                    tr.op("dve", lambda oc=oc: nc.vector.tensor_tensor(out=yconvT[:, oc, :], in0=acc[:], in1=cbT[:, oc, :], op=ALU.mult),
                          reads=[B_acc, B_cbT], writes=[B_ycv])
                B_o = Buf("cvo"); out_bufs.append(B_o)
                tr.dma("sp", lambda k=k: nc.sync.dma_start(out=conv_o[k], in_=pext[:, :, 512:514]), reads=[B_pext], writes=[B_o], key="cvo")
                for t in range(4):
                    pb, Bpb = pairbank()
                    for h in range(4):
                        tr.op("pe", lambda h=h, t=t, pb=pb: nc.tensor.matmul(pb[:, h * 256:(h + 1) * 256], lhsT=xqT[:, h, t * 128:(t + 1) * 128],
                                                                             rhs=mkT[:, h, :], start=True, stop=True),
                              reads=[B_xqT, B_mkv], writes=[Bpb[h // 2]])
                    tr.op("dve", lambda pb=pb: nc.vector.tensor_reduce(out=smx[:, 0:4], in_=pb.rearrange("p (h m) -> p h m", m=256), axis=AX.X, op=ALU.max),
                          reads=Bpb, writes=[B_smx])
                    tr.op("dve", lambda: nc.vector.tensor_scalar(out=smx[:, 4:8], in0=smx[:, 0:4], scalar1=-XS, scalar2=None, op0=ALU.mult),
                          reads=[B_smx], writes=[B_smx])
                    for h in range(4):
                        tr.op("act", lambda h=h, pb=pb: nc.scalar.activation(out=Pf[:, h, :], in_=pb[:, h * 256:(h + 1) * 256], func=AF.Exp,
                                                                             bias=smx[:, 4 + h:5 + h], scale=XS, accum_out=smx[:, 8 + h:9 + h]),
                              reads=Bpb + [B_smx], writes=[B_Pf, B_smx])
                    tr.op("dve", lambda: nc.vector.reciprocal(out=smx[:, 12:16], in_=smx[:, 8:12]), reads=[B_smx], writes=[B_smx])
                    tr.op("dve", lambda: nc.vector.tensor_tensor(out=Pn[:], in0=Pf[:], in1=smx[:, 12:16].unsqueeze(2).to_broadcast([128, 4, 256]),
                                                                 op=ALU.mult), reads=[B_Pf, B_smx], writes=[B_Pn])
                    bk, Bbk = nextbank()
                    bkb = bk.bitcast(BF16)
                    for h in range(4):
                        for mc in range(2):
                            tr.op("pe", lambda h=h, mc=mc, bkb=bkb: nc.tensor.transpose(out=bkb[:, (2 * h + mc) * 128:(2 * h + mc + 1) * 128],
                                                                                        in_=Pn[:, h, mc * 128:(mc + 1) * 128], identity=ident[:, :]),
                                  reads=[B_Pn, B_const], writes=[Bbk])
                    evac(PT[:], bkb.rearrange("p (j t) -> p j t", t=128), [Bbk], [B_PT])
                    bk, Bbk = nextbank()
                    for h in range(4):
                        for mc in range(2):
                            tr.op("pe", lambda h=h, mc=mc, bk=bk: nc.tensor.matmul(bk[:, h * 128:(h + 1) * 128], lhsT=mvb[:, mc, h * 128:(h + 1) * 128],
                                                                                   rhs=PT[:, 2 * h + mc, :], start=(mc == 0), stop=(mc == 1)),
                                  reads=[B_PT, B_mkv], writes=[Bbk])
                    evac(yxaT[:, :, t * 128:(t + 1) * 128], bk.rearrange("p (h t) -> p h t", t=128), [Bbk], [B_yxa])
                ybr = [(ysbT, B_ysbT), (yconvT, B_ycv), (yxaT, B_yxa)]
                for half in range(2):
                    for br in range(3):
                        g3, Bg = next_chunk(4 + half * 3 + br)
                        b3, Bb = next_chunk(10 + half * 3 + br)
                        yb, Byb = ybr[br]
                        for ocl in range(4):
                            oc = half * 4 + ocl
                            bk, Bbk = nextbank()
                            mm8(bk, Bbk, lambda c, ocl=ocl: g3[:, c, ocl * 128:(ocl + 1) * 128], lambda c: X0[:, c, :], [Bg, BX0])
                            tr.op("act", lambda bk=bk, br=br, oc=oc: nc.scalar.activation(out=gsb[:], in_=bk, func=AF.Sigmoid,
                                                                                         bias=bgt[:, br * 8 + oc:br * 8 + oc + 1], scale=1.0),
                                  reads=[Bbk, B_const], writes=[B_gsb])
                            bk2, Bbk2 = nextbank()
                            mm8(bk2, Bbk2, lambda c, ocl=ocl: b3[:, c, ocl * 128:(ocl + 1) * 128], lambda c, yb=yb: yb[:, c, :], [Bb, Byb], n=4)
                            if br == 0:
                                tr.op("dve", lambda bk2=bk2, ocl=ocl: nc.vector.tensor_tensor(out=macc[:, ocl, :], in0=gsb[:], in1=bk2, op=ALU.mult),
                                      reads=[Bbk2, B_gsb], writes=[B_macc])
                            else:
                                tr.op("dve", lambda bk2=bk2: nc.vector.tensor_tensor(out=acc[:], in0=gsb[:], in1=bk2, op=ALU.mult),
                                      reads=[Bbk2, B_gsb], writes=[B_acc])
                                if br == 1:
                                    tr.op("dve", lambda ocl=ocl: nc.vector.tensor_tensor(out=macc[:, ocl, :], in0=macc[:, ocl, :], in1=acc[:], op=ALU.add),
                                          reads=[B_acc, B_macc], writes=[B_macc])
                                else:
                                    tr.op("dve", lambda ocl=ocl, oc=oc: nc.vector.tensor_tensor(out=mT[:, oc, :], in0=macc[:, ocl, :], in1=acc[:], op=ALU.add),
                                          reads=[B_acc, B_macc], writes=[B_mT])
                w0, Bw0 = next_chunk(16)
                w1, Bw1 = next_chunk(17)
                for t in range(4):
                    pb, Bpb = pairbank()
                    for half, (w3, Bw3) in enumerate(((w0, Bw0), (w1, Bw1))):
                        mm8(pb[:, half * 512:(half + 1) * 512], Bpb[half], lambda c, t=t: mT[:, c, t * 128:(t + 1) * 128],
                            lambda c, w3=w3: w3[:, c, :], [B_mT, Bw3])
                    i = state["tile"]; state["tile"] += 1
                    sc, BS = rstd_of(pb, Bpb, 128, i)
                    tr.op("dve", lambda pb=pb, sc=sc: nc.vector.scalar_tensor_tensor(out=xt[0][:], in0=pb, scalar=sc, in1=gpost_b[:],
                                                                                     op0=ALU.mult, op1=ALU.mult),
                          reads=Bpb + [BS, B_const], writes=[B_xt[0]])
                    tr.op("dve", lambda t=t: nc.vector.tensor_tensor(out=xres[:, t, :], in0=xt[0][:], in1=xres[:, t, :], op=ALU.add),
                          reads=[B_xt[0], B_xres], writes=[B_xres])
                for t in range(4):
                    norm_T(xres[:, t, :], [B_xres], 128, HN, BHN, t * 128, gfpreT)
                for j in range(8):
                    u3, Bu = next_chunk(18 + j)
                    for ocl in range(4):
                        fc = 4 * j + ocl
                        bk, Bbk = nextbank()
                        mm8(bk, Bbk, lambda c, ocl=ocl: u3[:, c, ocl * 128:(ocl + 1) * 128], lambda c: HN[:, c, :], [Bu, BHN])
                        tr.op("act", lambda bk=bk: nc.scalar.activation(out=sqb[:], in_=bk, func=AF.Square), reads=[Bbk], writes=[B_sqb])
                        tr.op("dve", lambda bk=bk, fc=fc: nc.vector.scalar_tensor_tensor(out=f1T[:, fc, :], in0=bk, scalar=0.0, in1=sqb[:],
                                                                                         op0=ALU.is_gt, op1=ALU.mult),
                              reads=[Bbk, B_sqb], writes=[B_f1T])
                for half in range(2):
                    b4 = [nextbank() for _ in range(4)]
                    for kg in range(4):
                        d3, Bd = next_chunk(26 + half * 4 + kg)
                        for t in range(4):
                            for kc in range(8):
                                tr.op("pe", lambda t=t, kc=kc, kg=kg: nc.tensor.matmul(b4[t][0], lhsT=f1T[:, kg * 8 + kc, t * 128:(t + 1) * 128],
                                                                                       rhs=d3[:, kc, :], start=(kg == 0 and kc == 0),
                                                                                       stop=(kg == 3 and kc == 7)),
                                      reads=[B_f1T, Bd], writes=[b4[t][1]])
                    if half == 0:
                        for t in range(4):
                            evac(fsb[:, t, :], b4[t][0], [b4[t][1]], [B_fsb])
                    else:
                        for t in range(4):
                            i = state["tile"]; state["tile"] += 1
                            BS = B_ss[i % 4]
                            sa = ss[:, 2 * (i % 4):2 * (i % 4) + 1]; sb2 = ss[:, 2 * (i % 4) + 1:2 * (i % 4) + 2]
                            XN = xn[i % 2]; BXN = B_xn[i % 2]
                            tr.op("act", lambda t=t: nc.scalar.activation(out=XN[:, 0:512], in_=fsb[:, t, :], func=AF.Square, accum_out=sa),
                                  reads=[B_fsb], writes=[BXN, BS])
                            tr.op("act", lambda t=t: nc.scalar.activation(out=XN[:, 512:1024], in_=b4[t][0], func=AF.Square, accum_out=sb2),
                                  reads=[b4[t][1]], writes=[BXN, BS])
                            tr.op("dve", lambda: nc.vector.tensor_tensor(out=sa, in0=sa, in1=sb2, op=ALU.add), reads=[BS], writes=[BS])
                            tr.op("act", lambda: nc.scalar.activation(out=sb2, in_=sa, func=AF.Ln, bias=float(EPS), scale=1.0 / D), reads=[BS], writes=[BS])
                            tr.op("act", lambda: nc.scalar.activation(out=sa, in_=sb2, func=AF.Exp, scale=-0.5), reads=[BS], writes=[BS])
                            tr.op("dve", lambda t=t: nc.vector.scalar_tensor_tensor(out=xt[0][:, 0:512], in0=fsb[:, t, :], scalar=sa, in1=gfpost_b[:, 0:512],
                                                                                    op0=ALU.mult, op1=ALU.mult), reads=[B_fsb, BS, B_const], writes=[B_xt[0]])
                            tr.op("dve", lambda t=t: nc.vector.scalar_tensor_tensor(out=xt[0][:, 512:1024], in0=b4[t][0], scalar=sa, in1=gfpost_b[:, 512:1024],
                                                                                    op0=ALU.mult, op1=ALU.mult), reads=[b4[t][1], BS, B_const], writes=[B_xt[0]])
                            tr.op("dve", lambda t=t: nc.vector.tensor_tensor(out=xt[0][:], in0=xt[0][:], in1=xres[:, t, :], op=ALU.add),
                                  reads=[B_xt[0], B_xres], writes=[B_xt[0]])
                            B_o = Buf("yo"); out_bufs.append(B_o)
                            r0 = k * 512 + t * 128
                            tr.dma("sp", lambda r0=r0: nc.sync.dma_start(out=y_o[r0:r0 + 128, :], in_=xt[0][:]), reads=[B_xt[0]], writes=[B_o], key="yo")
            if True:
                X0 = xnT[0]; BX0 = B_xnT[0]; HN = xnT[1]; BHN = B_xnT[1]
                NT = 4
                stT = c3.sb([128, 4, 2, 4], F32); B_stT = Buf("stT")
                onesf = c3.sb([4, 128], F32); B_ones = Buf("ones4")
                xq_tok = gsb[0:4, :]; Qdiag = macc[0:4, :, :]; xqrep = fsb; Mk = ccT[:, 0:2, :]
                B_xqtok = B_gsb; B_Qd = B_macc; B_xq = B_fsb; B_Mk = B_ccT
                Sx = c3.sb([128, 2, 4, 4], F32); B_Sx = Buf("Sx")
                PnT = c3.sb([128, 2, 16], F32); B_PnT = Buf("PnT")
                front_tile(xs4[0:4, :], 4, X0, BX0, 0, gpreT)
                tr.dma("sp", lambda: nc.sync.dma_start(out=xres[0:4, 0, :], in_=xs4[0:4, :]), writes=[B_xres])
                tr.dma("pool", lambda: nc.gpsimd.dma_start(out=ysbT[:, :, 0:4], in_=ysbT_s), writes=[B_ysbT])
                tr.dma("sp", lambda: nc.sync.dma_start(out=stT[:], in_=stT_d), writes=[B_stT])
                tr.op("dve", lambda: nc.vector.memset(onesf[:], 1.0), writes=[B_ones])
                for which, dst in ((0, ks_o), (1, vs_o)):
                    ch3, Bch = next_chunk(34 + which)
                    bk, Bbk = nextbank()
                    mm8(bk[0:4, :], Bbk, lambda c: X0[:, c, 0:4], lambda c: ch3[:, c, :], [Bch, BX0])
                    evac(fsb[0:4, which, :], bk[0:4, :], [Bbk], [B_fsb])
                    B_o = Buf("kso"); out_bufs.append(B_o)
                    tr.dma("sp", lambda dst=dst, which=which: nc.sync.dma_start(out=dst, in_=fsb[0:4, which, :]), reads=[B_fsb], writes=[B_o], key="kso%d" % which)
                ch3, Bch = next_chunk(1)
                for oc in range(4):
                    bk, Bbk = nextbank()
                    mm8(bk[:, 0:NT], Bbk, lambda c, oc=oc: ch3[:, c, oc * 128:(oc + 1) * 128], lambda c: X0[:, c, 0:NT], [Bch, BX0])
                    evac(ccT[:, oc, 0:NT], bk[:, 0:NT], [Bbk], [B_ccT])
                ch3, Bch = next_chunk(2)
                for oc in range(4):
                    bk, Bbk = nextbank()
                    mm8(bk[:, 0:NT], Bbk, lambda c, oc=oc: ch3[:, c, oc * 128:(oc + 1) * 128], lambda c: X0[:, c, 0:NT], [Bch, BX0])
                    tr.op("dve", lambda oc=oc, bk=bk: nc.vector.tensor_tensor(out=pext[:, oc, 0:NT], in0=ccT[:, oc, 0:NT], in1=bk[:, 0:NT], op=ALU.mult),
                          reads=[Bbk, B_ccT], writes=[B_pext])
                ch3, Bch = next_chunk(0)
                for oc in range(4):
                    bk, Bbk = nextbank()
                    mm8(bk[:, 0:NT], Bbk, lambda c, oc=oc: ch3[:, c, oc * 128:(oc + 1) * 128], lambda c: X0[:, c, 0:NT], [Bch, BX0])
                    evac(cbT[:, oc, 0:NT], bk[:, 0:NT], [Bbk], [B_cbT])
                ch3, Bch = next_chunk(3)
                for oc in range(4):
                    bk, Bbk = nextbank()
                    mm8(bk[:, 0:NT], Bbk, lambda c, oc=oc: ch3[:, c, oc * 128:(oc + 1) * 128], lambda c: X0[:, c, 0:NT], [Bch, BX0])
                    evac(xqT[:, oc, 0:NT], bk[:, 0:NT], [Bbk], [B_xqT])
                bk, Bbk = nextbank()
                mm8(bk[0:4, :], Bbk, lambda c: X0[:, c, 0:4], lambda c: ch3[:, c, :], [Bch, BX0])
                evac(xq_tok, bk[0:4, :], [Bbk], [B_xqtok])
                for oc in range(4):
                    tr.op("dve", lambda oc=oc: nc.vector.tensor_scalar(out=acc[:, 0:NT], in0=stT[:, oc, 0, :], scalar1=wcv[:, oc, 0:1], scalar2=None,
                                                                       op0=ALU.mult), reads=[B_stT, B_const], writes=[B_acc])
                    tr.op("dve", lambda oc=oc: nc.vector.scalar_tensor_tensor(out=acc[:, 0:NT], in0=stT[:, oc, 1, :], scalar=wcv[:, oc, 1:2],
                                                                              in1=acc[:, 0:NT], op0=ALU.mult, op1=ALU.add),
                          reads=[B_stT, B_const, B_acc], writes=[B_acc])
                    tr.op("dve", lambda oc=oc: nc.vector.scalar_tensor_tensor(out=acc[:, 0:NT], in0=pext[:, oc, 0:NT], scalar=wcv[:, oc, 2:3],
                                                                              in1=acc[:, 0:NT], op0=ALU.mult, op1=ALU.add),
                          reads=[B_pext, B_const, B_acc], writes=[B_acc])
                    tr.op("dve", lambda oc=oc: nc.vector.tensor_tensor(out=yconvT[:, oc, 0:NT], in0=acc[:, 0:NT], in1=cbT[:, oc, 0:NT], op=ALU.mult),
                          reads=[B_acc, B_cbT], writes=[B_ycv])
                B_o = Buf("cso"); out_bufs.append(B_o)
                tr.dma("sp", lambda: nc.sync.dma_start(out=cs_o[:, :, 0, :], in_=stT[:, :, 1, :]), reads=[B_stT], writes=[B_o], key="cso0")
                B_o = Buf("cso1"); out_bufs.append(B_o)
                tr.dma("sp", lambda: nc.sync.dma_start(out=cs_o[:, :, 1, :], in_=pext[:, :, 0:NT]), reads=[B_pext], writes=[B_o], key="cso1")
                tr.op("dve", lambda: nc.vector.tensor_tensor(out=Qdiag, in0=xq_tok.unsqueeze(1).to_broadcast([4, 4, 512]),
                                                             in1=identf[0:4, 0:4].unsqueeze(2).to_broadcast([4, 4, 512]), op=ALU.mult),
                      reads=[B_xqtok, B_const], writes=[B_Qd])
                for s in range(4):
                    bk, Bbk = nextbank()
                    tr.op("pe", lambda s=s, bk=bk: nc.tensor.matmul(bk, lhsT=onesf[0:4, :], rhs=Qdiag[:, s, :], start=True, stop=True),
                          reads=[B_Qd, B_ones], writes=[Bbk])
                    evac(xqrep[:, s, :], bk, [Bbk], [B_xq])
                for s in range(4):
                    tr.dma("sp", lambda s=s: nc.sync.dma_start(out=Mk, in_=cmk[s].rearrange("(mc p) n -> p mc n", p=128)), writes=[B_Mk])
                    tr.op("dve", lambda s=s: nc.vector.tensor_tensor(out=Mk, in0=Mk, in1=xqrep[:, s:s + 1, :].to_broadcast([128, 2, 512]), op=ALU.mult),
                          reads=[B_Mk, B_xq], writes=[B_Mk])
                    tr.op("dve", lambda s=s: nc.vector.tensor_reduce(out=Sx[:, :, s, :], in_=Mk.rearrange("p mc (h d) -> p mc h d", d=128),
                                                                     axis=AX.X, op=ALU.add), reads=[B_Mk], writes=[B_Sx])
                pb, Bpb = pairbank()
                for mc in range(2):
                    tr.op("pe", lambda mc=mc, pb=pb: nc.tensor.transpose(out=pb[0:16, mc * 128:(mc + 1) * 128],
                                                                         in_=Sx[:, mc, :, :].rearrange("p s h -> p (s h)"), identity=identf[:, :]),
                          reads=[B_Sx, B_const], writes=[Bpb[0]])
                ST = pb[0:16, 0:256]
                tr.op("dve", lambda: nc.vector.tensor_reduce(out=smx[0:16, 0:1], in_=ST, axis=AX.X, op=ALU.max), reads=[Bpb[0]], writes=[B_smx])
                tr.op("dve", lambda: nc.vector.tensor_scalar(out=smx[0:16, 4:5], in0=smx[0:16, 0:1], scalar1=-XS, scalar2=None, op0=ALU.mult),
                      reads=[B_smx], writes=[B_smx])
                tr.op("act", lambda: nc.scalar.activation(out=Pf[0:16, 0, :], in_=ST, func=AF.Exp, bias=smx[0:16, 4:5], scale=XS,
                                                          accum_out=smx[0:16, 8:9]), reads=[Bpb[0], B_smx], writes=[B_Pf, B_smx])
                tr.op("dve", lambda: nc.vector.reciprocal(out=smx[0:16, 12:13], in_=smx[0:16, 8:9]), reads=[B_smx], writes=[B_smx])
                tr.op("dve", lambda: nc.vector.tensor_scalar(out=Pf[0:16, 1, :], in0=Pf[0:16, 0, :], scalar1=smx[0:16, 12:13], scalar2=None, op0=ALU.mult),
                      reads=[B_Pf, B_smx], writes=[B_Pf])
                bk, Bbk = nextbank()
                for mc in range(2):
                    tr.op("pe", lambda mc=mc, bk=bk: nc.tensor.transpose(out=bk[:, mc * 16:(mc + 1) * 16], in_=Pf[0:16, 1, mc * 128:(mc + 1) * 128],
                                                                         identity=identf[0:16, 0:16]), reads=[B_Pf, B_const], writes=[Bbk])
                evac(PnT[:], bk[:, 0:32].rearrange("p (mc j) -> p mc j", j=16), [Bbk], [B_PnT])
                bko, Bbko = nextbank()
                for s in range(4):
                    tr.dma("sp", lambda s=s: nc.sync.dma_start(out=Mk, in_=cmv[s].rearrange("(mc p) n -> p mc n", p=128)), writes=[B_Mk])
                    for h in range(4):
                        for mc in range(2):
                            tr.op("pe", lambda s=s, h=h, mc=mc: nc.tensor.matmul(bko[:, h * 4 + s:h * 4 + s + 1], lhsT=Mk[:, mc, h * 128:(h + 1) * 128],
                                                                                 rhs=PnT[:, mc, s * 4 + h:s * 4 + h + 1], start=(mc == 0), stop=(mc == 1)),
                                  reads=[B_Mk, B_PnT], writes=[Bbko])
                evac(yxaT[:, :, 0:NT], bko[:, 0:16].rearrange("p (h s) -> p h s", s=4), [Bbko], [B_yxa])
                ybr = [(ysbT, B_ysbT), (yconvT, B_ycv), (yxaT, B_yxa)]
                for half in range(2):
                    for br in range(3):
                        g3, Bg = next_chunk(4 + half * 3 + br)
                        b3, Bb = next_chunk(10 + half * 3 + br)
                        yb, Byb = ybr[br]
                        for ocl in range(4):
                            oc = half * 4 + ocl
                            bk, Bbk = nextbank()
                            mm8(bk[:, 0:NT], Bbk, lambda c, ocl=ocl: g3[:, c, ocl * 128:(ocl + 1) * 128], lambda c: X0[:, c, 0:NT], [Bg, BX0])
                            tr.op("act", lambda bk=bk, br=br, oc=oc: nc.scalar.activation(out=gsb[:, 0:NT], in_=bk[:, 0:NT], func=AF.Sigmoid,
                                                                                         bias=bgt[:, br * 8 + oc:br * 8 + oc + 1], scale=1.0),
                                  reads=[Bbk, B_const], writes=[B_gsb])
                            bk2, Bbk2 = nextbank()
                            mm8(bk2[:, 0:NT], Bbk2, lambda c, ocl=ocl: b3[:, c, ocl * 128:(ocl + 1) * 128], lambda c, yb=yb: yb[:, c, 0:NT], [Bb, Byb], n=4)
                            if br == 0:
                                tr.op("dve", lambda bk2=bk2, ocl=ocl: nc.vector.tensor_tensor(out=macc[:, ocl, 0:NT], in0=gsb[:, 0:NT], in1=bk2[:, 0:NT], op=ALU.mult),
                                      reads=[Bbk2, B_gsb], writes=[B_macc])
                            else:
                                tr.op("dve", lambda bk2=bk2: nc.vector.tensor_tensor(out=acc[:, 0:NT], in0=gsb[:, 0:NT], in1=bk2[:, 0:NT], op=ALU.mult),
                                      reads=[Bbk2, B_gsb], writes=[B_acc])
                                if br == 1:
                                    tr.op("dve", lambda ocl=ocl: nc.vector.tensor_tensor(out=macc[:, ocl, 0:NT], in0=macc[:, ocl, 0:NT], in1=acc[:, 0:NT], op=ALU.add),
                                          reads=[B_acc, B_macc], writes=[B_macc])
                                else:
                                    tr.op("dve", lambda ocl=ocl, oc=oc: nc.vector.tensor_tensor(out=mT[:, oc, 0:NT], in0=macc[:, ocl, 0:NT], in1=acc[:, 0:NT], op=ALU.add),
                                          reads=[B_acc, B_macc], writes=[B_mT])
                w0, Bw0 = next_chunk(16)
                w1, Bw1 = next_chunk(17)
                pb, Bpb = pairbank()
                for half, (w3, Bw3) in enumerate(((w0, Bw0), (w1, Bw1))):
                    mm8(pb[0:4, half * 512:(half + 1) * 512], Bpb[half], lambda c: mT[:, c, 0:4], lambda c, w3=w3: w3[:, c, :], [B_mT, Bw3])
                i = state["tile"]; state["tile"] += 1
                sc, BS = rstd_of(pb[0:4, :], Bpb, 4, i)
                tr.op("dve", lambda pb=pb, sc=sc: nc.vector.scalar_tensor_tensor(out=xt[0][0:4, :], in0=pb[0:4, :], scalar=sc, in1=gpost_b[0:4, :],
                                                                                 op0=ALU.mult, op1=ALU.mult), reads=Bpb + [BS, B_const], writes=[B_xt[0]])
                tr.op("dve", lambda: nc.vector.tensor_tensor(out=xres[0:4, 0, :], in0=xt[0][0:4, :], in1=xres[0:4, 0, :], op=ALU.add),
                      reads=[B_xt[0], B_xres], writes=[B_xres])
                norm_T(xres[0:4, 0, :], [B_xres], 4, HN, BHN, 0, gfpreT)
                for j in range(8):
                    u3, Bu = next_chunk(18 + j)
                    for ocl in range(4):
                        fc = 4 * j + ocl
                        bk, Bbk = nextbank()
                        mm8(bk[:, 0:NT], Bbk, lambda c, ocl=ocl: u3[:, c, ocl * 128:(ocl + 1) * 128], lambda c: HN[:, c, 0:NT], [Bu, BHN])
                        tr.op("act", lambda bk=bk: nc.scalar.activation(out=sqb[:, 0:NT], in_=bk[:, 0:NT], func=AF.Square), reads=[Bbk], writes=[B_sqb])
                        tr.op("dve", lambda bk=bk, fc=fc: nc.vector.scalar_tensor_tensor(out=f1T[:, fc, 0:NT], in0=bk[:, 0:NT], scalar=0.0, in1=sqb[:, 0:NT],
                                                                                         op0=ALU.is_gt, op1=ALU.mult), reads=[Bbk, B_sqb], writes=[B_f1T])
                pb, Bpb = pairbank()
                for half in range(2):
                    for kg in range(4):
                        d3, Bd = next_chunk(26 + half * 4 + kg)
                        for kc in range(8):
                            tr.op("pe", lambda kc=kc, kg=kg, half=half, d3=d3: nc.tensor.matmul(pb[0:4, half * 512:(half + 1) * 512], lhsT=f1T[:, kg * 8 + kc, 0:4],
                                                                                               rhs=d3[:, kc, :], start=(kg == 0 and kc == 0),
                                                                                               stop=(kg == 3 and kc == 7)),
                                  reads=[B_f1T, Bd], writes=[Bpb[half]])
                i = state["tile"]; state["tile"] += 1
                sc, BS = rstd_of(pb[0:4, :], Bpb, 4, i)
                tr.op("dve", lambda pb=pb, sc=sc: nc.vector.scalar_tensor_tensor(out=xt[0][0:4, :], in0=pb[0:4, :], scalar=sc, in1=gfpost_b[0:4, :],
                                                                                 op0=ALU.mult, op1=ALU.mult), reads=Bpb + [BS, B_const], writes=[B_xt[0]])
                tr.op("dve", lambda: nc.vector.tensor_tensor(out=xt[0][0:4, :], in0=xt[0][0:4, :], in1=xres[0:4, 0, :], op=ALU.add),
                      reads=[B_xt[0], B_xres], writes=[B_xt[0]])
                B_o = Buf("yso"); out_bufs.append(B_o)
                tr.dma("sp", lambda: nc.sync.dma_start(out=ys_o, in_=xt[0][0:4, :]), reads=[B_xt[0]], writes=[B_o], key="yso")
            barrier()
        tr.finish(out_bufs)
    return nc


def _consts():
    p = np.arange(128)
    tri = (p[:, None] >= p[None, :]).astype(np.float32)
    bf = ml_dtypes.bfloat16
    return dict(ident=np.eye(128, dtype=np.float32).astype(bf), tri=tri.astype(bf), fixm=(1 - tri).astype(bf),
                identf=np.eye(128, dtype=np.float32))


def main_inputs(inp, c, G):
    b, i = c // 4, c % 4
    og = own_groups(i, G)
    xp = inp["x_prompt"]
    d = dict(_consts())
    d["xb"] = np.ascontiguousarray(xp[b])
    d["xo"] = np.concatenate([xp[b, g * 512:(g + 1) * 512] for g in og])
    xh = np.zeros((8, D), np.float32)
    for k, g in enumerate(og):
        if g > 0:
            xh[2 * k:2 * k + 2] = xp[b, g * 512 - 2:g * 512]
    d["xh"] = xh
    d["maskwin"] = make_maskwin(i)
    d["mem"] = np.ascontiguousarray(inp["mem_prompt"][b])
    gT = lambda a: np.ascontiguousarray(np.asarray(a).reshape(8, 128).T)
    d["g_mix_preT"] = gT(inp["g_mix_pre"][0]); d["g_ffn_preT"] = gT(inp["g_ffn_pre"][0]); d["g_memT"] = gT(inp["g_mem"][0])
    d["g_mix_post"] = np.asarray(inp["g_mix_post"]).reshape(1, D); d["g_ffn_post"] = np.asarray(inp["g_ffn_post"]).reshape(1, D)
    d["b_sb"] = np.asarray(inp["b_sb"]).reshape(1, NH)
    for n in ["w_in", "w_mem_kv", "w_gate", "w_sb_o", "w_conv_o", "w_xa_o", "w_o", "w_up", "w_down"]:
        d[n] = np.ascontiguousarray(np.asarray(inp[n])[0])
    d["wconvT"] = np.ascontiguousarray(np.asarray(inp["w_conv"])[0].reshape(3, 4, 128).transpose(2, 1, 0))
    d["b_gateT"] = np.ascontiguousarray(np.asarray(inp["b_gate"])[0].reshape(24, 128).T)
    return d


def build_sattn(NPHYS, NSEQ=32):
    nc = bass.Bass("TRN2", target_bir_lowering=False)
    din = lambda n, s, d=F32: nc.dram_tensor(n, list(s), d, kind="ExternalInput").ap()
    dout = lambda n, s, d=F32: nc.dram_tensor(n, list(s), d, kind="ExternalOutput").ap()
    kp = din("kp", [NPHYS, 128 * DH]); vp = din("vp", [NPHYS, 128 * DH])
    ptT = din("ptT", [128, NSEQ], I32)
    xs = din("xs", [NSEQ, D]); g_preT = din("g_mix_preT", [128, 8]); wq = din("wq", [D, DH]); bsb1 = din("bsb1", [1, 1])
    ident_d = din("ident", [128, 128], BF16); identf_d = din("identf", [128, 128], F32); strict_d = din("strictf", [128, 128], F32)
    ysb = dout("ysb", [1, NSEQ * DH])
    with contextlib.ExitStack() as es:
        tr = Tracker(nc, es); cx = Ctx(nc, es); out_bufs = []
        ident = cx.sb([128, 128], BF16); identf = cx.sb([128, 128], F32); strictf = cx.sb([128, 128], F32)
        gpreT = cx.sb([128, 8], F32); bsb = cx.sb([128, 1], F32); pt = cx.sb([128, NSEQ], I32)
        onesf = cx.sb([128, 128], F32)
        B_const = Buf("const")
        ld = lambda o, i: tr.dma("sp", lambda: nc.sync.dma_start(out=o, in_=i), writes=[B_const], key="const")
        ld(ident[:], ident_d); ld(identf[:], identf_d); ld(strictf[:], strict_d); ld(gpreT[:], g_preT)
        ld(bsb[:], bsb1.partition_broadcast(128)); ld(pt[:], ptT)
        tr.op("dve", lambda: nc.vector.memset(onesf[:], 1.0), writes=[B_const])
        bigs = [cx.ps([128, 1024], F32, name="big%d" % i) for i in range(4)]
        banks = [bigs[i // 2][:, (i % 2) * 512:(i % 2 + 1) * 512] for i in range(8)]
        Bbank = [Buf("bank%d" % i) for i in range(8)]
        xt = cx.sb([NSEQ, D], F32); xnb = cx.sb([NSEQ, D], BF16); ssm = cx.sb([NSEQ, 2], F32); B_x = Buf("x")
        xnT = cx.sb([128, 8, NSEQ], BF16); B_xnT = Buf("xnT")
        wqb = cx.sb([128, 8, DH], BF16); B_wq = Buf("wq")
        tr.dma("pool", lambda: nc.gpsimd.dma_start(out=wqb[:], in_=wq.rearrange("(c p) n -> p c n", p=128)), writes=[B_wq])
        tr.dma("sp", lambda: nc.sync.dma_start(out=xt[:], in_=xs), writes=[B_x])
        tr.op("act", lambda: nc.scalar.activation(out=xnb[:], in_=xt[:], func=AF.Square, accum_out=ssm[:, 0:1]), reads=[B_x], writes=[B_x])
        tr.op("act", lambda: nc.scalar.activation(out=ssm[:, 1:2], in_=ssm[:, 0:1], func=AF.Ln, bias=float(EPS), scale=1.0 / D), reads=[B_x], writes=[B_x])
        tr.op("act", lambda: nc.scalar.activation(out=ssm[:, 0:1], in_=ssm[:, 1:2], func=AF.Exp, scale=-0.5), reads=[B_x], writes=[B_x])
        tr.op("dve", lambda: nc.vector.tensor_scalar(out=xnb[:], in0=xt[:], scalar1=ssm[:, 0:1], scalar2=None, op0=ALU.mult), reads=[B_x], writes=[B_x])
        bkb = banks[0].bitcast(BF16)
        for c in range(8):
            tr.op("pe", lambda c=c: nc.tensor.transpose(out=bkb[:, c * 128:c * 128 + NSEQ], in_=xnb[:, c * 128:(c + 1) * 128],
                                                        identity=ident[0:NSEQ, 0:NSEQ]), reads=[B_x, B_const], writes=[Bbank[0]])
        tr.op("dve", lambda: nc.vector.tensor_tensor(out=xnT[:], in0=bkb.rearrange("p (c t) -> p c t", t=128)[:, :, 0:NSEQ],
                                                     in1=gpreT[:, :].unsqueeze(2).to_broadcast([128, 8, NSEQ]), op=ALU.mult),
              reads=[Bbank[0], B_const], writes=[B_xnT])
        for c in range(8):
            tr.op("pe", lambda c=c: nc.tensor.matmul(banks[1][0:NSEQ, 0:DH], lhsT=xnT[:, c, :], rhs=wqb[:, c, :], start=(c == 0), stop=(c == 7)),
                  reads=[B_xnT, B_wq], writes=[Bbank[1]])
        q_sb = cx.sb([NSEQ, DH], F32); Qdiag = cx.sb([NSEQ, NSEQ, DH], F32); qrep = cx.sb([128, NSEQ, DH], F32); B_q = Buf("q")
        tr.op("dve", lambda: nc.vector.tensor_copy(out=q_sb[:], in_=banks[1][0:NSEQ, 0:DH]), reads=[Bbank[1]], writes=[B_q])
        tr.op("dve", lambda: nc.vector.tensor_tensor(out=Qdiag[:], in0=q_sb[:].unsqueeze(1).to_broadcast([NSEQ, NSEQ, DH]),
                                                     in1=identf[0:NSEQ, 0:NSEQ].unsqueeze(2).to_broadcast([NSEQ, NSEQ, DH]), op=ALU.mult),
              reads=[B_q, B_const], writes=[B_q])
        nq = NSEQ * DH // 512
        for j in range(nq):
            tr.op("pe", lambda j=j: nc.tensor.matmul(banks[2 + j % 4], lhsT=onesf[0:NSEQ, :],
                                                     rhs=Qdiag[:].rearrange("a s d -> a (s d)")[:, j * 512:(j + 1) * 512], start=True, stop=True),
                  reads=[B_q, B_const], writes=[Bbank[2 + j % 4]])
            tr.op("dve", lambda j=j: nc.vector.tensor_copy(out=qrep[:].rearrange("p s d -> p (s d)")[:, j * 512:(j + 1) * 512], in_=banks[2 + j % 4]),
                  reads=[Bbank[2 + j % 4]], writes=[B_q])
        Kt = [cx.sb([128, 128, DH], F32) for _ in range(2)]; B_K = [Buf("K0"), Buf("K1")]
        Vt = [cx.sb([128, 128 * DH], F32) for _ in range(2)]; B_V = [Buf("V0"), Buf("V1")]
        Vb = [cx.sb([128, 128 * DH], BF16) for _ in range(2)]; B_Vb = [Buf("Vb0"), Buf("Vb1")]
        Z = cx.sb([128, 128], F32); Pre = cx.sb([128, 128], F32); Ee = cx.sb([128, 128], F32); Ll = cx.sb([128, 128], F32)
        Wt = [cx.sb([128, 128], BF16) for _ in range(2)]; B_W = [Buf("W0"), Buf("W1")]
        Asm = cx.sb([128, 2], F32)
        B_Z = Buf("Z"); B_E = Buf("E"); B_L = Buf("L"); B_P = Buf("Pre"); B_A = Buf("A")
        ybank = bigs[3]; By = [Bbank[6], Bbank[7]]
        ybank2 = bigs[2]; By2 = [Bbank[4], Bbank[5]]

        def gather(s):
            i = s % 2
            tr.dma("pool", lambda: nc.gpsimd.indirect_dma_start(out=Kt[i][:].rearrange("p t d -> p (t d)"), out_offset=None, in_=kp,
                                                                in_offset=bass.IndirectOffsetOnAxis(ap=pt[:, s:s + 1], axis=0)),
                   reads=[B_const], writes=[B_K[i]])
            tr.dma("pool", lambda: nc.gpsimd.indirect_dma_start(out=Vt[i][:], out_offset=None, in_=vp,
                                                                in_offset=bass.IndirectOffsetOnAxis(ap=pt[:, s:s + 1], axis=0)),
                   reads=[B_const], writes=[B_V[i]])
        gather(0)
        for s in range(NSEQ):
            i = s % 2
            if s + 1 < NSEQ:
                gather(s + 1)
            tr.op("dve", lambda: nc.vector.tensor_tensor(out=Kt[i][:], in0=Kt[i][:], in1=qrep[:, s:s + 1, :].to_broadcast([128, 128, DH]), op=ALU.mult),
                  reads=[B_K[i], B_q], writes=[B_K[i]])
            tr.op("dve", lambda: nc.vector.tensor_reduce(out=Z[:], in_=Kt[i][:], axis=AX.X, op=ALU.add), reads=[B_K[i]], writes=[B_Z])
            tr.op("act", lambda: nc.scalar.copy(out=Vb[i][:], in_=Vt[i][:]), reads=[B_V[i]], writes=[B_Vb[i]])
            tr.op("dve", lambda: nc.vector.tensor_scalar(out=Z[:], in0=Z[:], scalar1=0.125, scalar2=bsb[:, 0:1], op0=ALU.mult, op1=ALU.add),
                  reads=[B_Z, B_const], writes=[B_Z])
            tr.op("act", lambda: nc.scalar.activation(out=Ee[:], in_=Z[:], func=AF.Exp), reads=[B_Z], writes=[B_E])
            tr.op("act", lambda: nc.scalar.activation(out=Ll[:], in_=Ee[:], func=AF.Ln, bias=1.0, scale=1.0), reads=[B_E], writes=[B_L])
            tr.op("dve", lambda: nc.vector.tensor_tensor_scan(out=Pre[:], data0=onesf[:, :], data1=Ll[:], initial=0.0, op0=ALU.mult, op1=ALU.add),
                  reads=[B_L, B_const], writes=[B_P])
            tr.op("pe", lambda: nc.tensor.matmul(banks[0][:, 0:2], lhsT=strictf[:, :], rhs=Pre[:, 126:128], start=True, stop=True),
                  reads=[B_P, B_const], writes=[Bbank[0]])
            tr.op("dve", lambda: nc.vector.tensor_tensor(out=Asm[:, 0:1], in0=banks[0][:, 1:2], in1=Pre[:, 127:128], op=ALU.add),
                  reads=[Bbank[0], B_P], writes=[B_A])
            tr.op("dve", lambda: nc.vector.tensor_scalar(out=Asm[:, 1:2], in0=Asm[:, 0:1], scalar1=-1.0, scalar2=None, op0=ALU.mult),
                  reads=[B_A], writes=[B_A])
            tr.op("dve", lambda: nc.vector.tensor_tensor(out=Z[:], in0=Z[:], in1=Pre[:], op=ALU.add), reads=[B_Z, B_P], writes=[B_Z])
            tr.op("dve", lambda: nc.vector.tensor_tensor(out=Z[:], in0=Z[:], in1=Ll[:], op=ALU.subtract), reads=[B_Z, B_L], writes=[B_Z])
            tr.op("act", lambda: nc.scalar.activation(out=Wt[i][:], in_=Z[:], func=AF.Exp, bias=Asm[:, 1:2], scale=1.0),
                  reads=[B_Z, B_A], writes=[B_W[i]])
            yb, Byb = (ybank, By) if s < 16 else (ybank2, By2)
            col = (s % 16) * DH
            for t in range(128):
                tr.op("pe", lambda t=t: nc.tensor.matmul(yb[0:1, col:col + DH], lhsT=Wt[i][:, t:t + 1], rhs=Vb[i][:, t * DH:(t + 1) * DH],
                                                         start=(t == 0), stop=(t == 127)), reads=[B_W[i], B_Vb[i]], writes=[Byb[col // 512]])
        yrow = cx.sb([1, NSEQ * DH], F32); B_yr = Buf("yrow")
        n1 = min(NSEQ, 16) * DH
        tr.op("dve", lambda: nc.vector.tensor_copy(out=yrow[:, 0:n1], in_=ybank[0:1, 0:n1]), reads=By, writes=[B_yr])
        if NSEQ > 16:
            tr.op("dve", lambda: nc.vector.tensor_copy(out=yrow[:, n1:], in_=ybank2[0:1, 0:NSEQ * DH - n1]), reads=By2, writes=[B_yr])
        B_o = Buf("o"); out_bufs.append(B_o)
        tr.dma("sp", lambda: nc.sync.dma_start(out=ysb, in_=yrow[:]), reads=[B_yr], writes=[B_o])
        tr.finish(out_bufs)
    return nc


def sattn_inputs(inp, c, pt):
    bf = ml_dtypes.bfloat16
    p = np.arange(128)
    d = dict(ident=np.eye(128, dtype=np.float32).astype(bf), identf=np.eye(128, dtype=np.float32),
             strictf=(p[:, None] > p[None, :]).astype(np.float32))
    ck = np.asarray(inp["cache_k_pages"]); cv_ = np.asarray(inp["cache_v_pages"])
    nph = ck.shape[1]
    d["kp"] = np.ascontiguousarray(ck[0, :, :, c, :]).reshape(nph, 128 * DH)
    d["vp"] = np.ascontiguousarray(cv_[0, :, :, c, :]).reshape(nph, 128 * DH)
    d["ptT"] = np.ascontiguousarray(pt.T.astype(np.int32))
    d["xs"] = np.ascontiguousarray(np.asarray(inp["x_sample"])[:, 0, :])
    d["g_mix_preT"] = np.ascontiguousarray(np.asarray(inp["g_mix_pre"])[0].reshape(8, 128).T)
    d["wq"] = np.ascontiguousarray(np.asarray(inp["w_in"])[0][:, c * DH:(c + 1) * DH])
    d["bsb1"] = np.asarray(inp["b_sb"])[0, c:c + 1].reshape(1, 1).astype(np.float32)
    return d


def sample_inputs(inp, c, ysb_s):
    d = {}
    d["xs4"] = np.ascontiguousarray(np.asarray(inp["x_sample"])[4 * c:4 * c + 4, 0, :])
    y4 = ysb_s[4 * c:4 * c + 4]
    d["ysbT_s"] = np.ascontiguousarray(y4.reshape(4, 4, 128).transpose(2, 1, 0)).astype(np.float32)
    st = np.asarray(inp["state_conv"])[0, 4 * c:4 * c + 4]
    d["stT"] = np.ascontiguousarray(st.reshape(4, 2, 4, 128).transpose(3, 2, 1, 0))
    d["cmk"] = np.ascontiguousarray(np.asarray(inp["cache_mem_k"])[0, 4 * c:4 * c + 4].reshape(4, NMEM, 512))
    d["cmv"] = np.ascontiguousarray(np.asarray(inp["cache_mem_v"])[0, 4 * c:4 * c + 4].reshape(4, NMEM, 512))
    return d


_NC_CACHE = {}


def kernel(**inp):
    inp = {k: np.asarray(v) for k, v in inp.items()}
    SEQ = inp["x_prompt"].shape[1]
    G = SEQ // 512
    NS = G // 4
    nph = inp["cache_k_pages"].shape[1]
    nseq = inp["x_sample"].shape[0]
    pt = inp["page_table"]
    if ("A", nph, nseq) not in _NC_CACHE:
        _NC_CACHE[("A", nph, nseq)] = build_sattn(nph, nseq)
    ncA = _NC_CACHE[("A", nph, nseq)]
    resA = run_bass_kernel_spmd(ncA, [sattn_inputs(inp, c, pt) for c in range(8)], core_ids=list(range(8)))
    ysb_s = np.concatenate([np.asarray(resA.results[c]["ysb"]).reshape(nseq, DH) for c in range(8)], axis=1)
    if ("B", G) not in _NC_CACHE:
        _NC_CACHE[("B", G)] = build_main(G)
    ncB = _NC_CACHE[("B", G)]
    ins = []
    for c in range(8):
        d = main_inputs(inp, c, G)
        d.update(sample_inputs(inp, c, ysb_s))
        ins.append(d)
    res = run_bass_kernel_spmd(ncB, ins, core_ids=list(range(8))).results
    f32 = np.float32
    y_p = np.zeros((2, SEQ, D), f32); k_p = np.zeros((1, 2, SEQ, NH, DH), f32); v_p = np.zeros((1, 2, SEQ, NH, DH), f32)
    conv_p = np.zeros((1, 2, 2, CW), f32); mk_p = np.zeros((1, 2, NMEM, XAH, XAD), f32); mv_p = np.zeros((1, 2, NMEM, XAH, XAD), f32)
    y_s = np.zeros((nseq, 1, D), f32); k_s = np.zeros((1, nseq, 1, NH, DH), f32); v_s = np.zeros((1, nseq, 1, NH, DH), f32)
    conv_s = np.zeros((1, nseq, 2, CW), f32)
    for c in range(8):
        b, i = c // 4, c % 4
        r = res[c]
        for k, g in enumerate(own_groups(i, G)):
            sl = slice(g * 512, (g + 1) * 512)
            y_p[b, sl] = r["y_o"][k * 512:(k + 1) * 512]
            k_p[0, b, sl] = np.asarray(r["k_o"][k * 512:(k + 1) * 512]).reshape(512, NH, DH)
            v_p[0, b, sl] = np.asarray(r["v_o"][k * 512:(k + 1) * 512]).reshape(512, NH, DH)
            if g == G - 1:
                conv_p[0, b] = np.asarray(r["conv_o"])[k].transpose(2, 1, 0).reshape(2, CW)
        if i == 0:
            mk_p[0, b] = np.asarray(r["mk_o"]).reshape(NMEM, XAH, XAD)
            mv_p[0, b] = np.asarray(r["mv_o"]).reshape(NMEM, XAH, XAD)
        y_s[4 * c:4 * c + 4, 0] = r["ys_o"]
        k_s[0, 4 * c:4 * c + 4, 0] = np.asarray(r["ks_o"]).reshape(4, NH, DH)
        v_s[0, 4 * c:4 * c + 4, 0] = np.asarray(r["vs_o"]).reshape(4, NH, DH)
        conv_s[0, 4 * c:4 * c + 4] = np.asarray(r["cs_o"]).transpose(3, 2, 1, 0).reshape(4, 2, CW)
    return (y_p, y_s, k_p, v_p, conv_p, mk_p, mv_p, k_s, v_s, conv_s)
```

```python
import contextlib
import math
import numpy as np
import ml_dtypes
import concourse.bass as bass
import concourse.mybir as mybir
from concourse.bass_utils import run_bass_kernel_spmd

F32 = mybir.dt.float32
BF16 = mybir.dt.bfloat16
I32 = mybir.dt.int32
AF = mybir.ActivationFunctionType
ALU = mybir.AluOpType
AX = mybir.AxisListType

D = 1024
NH = 8
DH = 64
SBW = 512
CW = 512
XAH = 4
XAD = 128
NMEM = 256
DFF = 4096
INC = 3584
EPS = 1e-6
NEG = -30000.0


class Buf:
    def __init__(self, name):
        self.name = name
        self.w = None
        self.r = {}


class Tracker:
    def __init__(self, nc, es):
        self.nc = nc
        self.es = es
        self.eng = {"pe": nc.tensor, "act": nc.scalar, "dve": nc.vector, "pool": nc.gpsimd, "sp": nc.sync}
        self.sem = {k: es.enter_context(nc.semaphore("sem_" + k)) for k in ["pe", "act", "dve", "pool"]}
        self.cnt = dict.fromkeys(self.sem, 0)
        self.waited = {}
        self.dsem = {}
        self.dcnt = {}
        self.nd = 0

    def _wait(self, waiter, ev):
        kind, key, val = ev
        if kind == "eng" and key == "pe" and waiter == "pe":
            return
        k = (waiter, kind, key)
        if val <= self.waited.get(k, 0):
            return
        sem = self.sem[key] if kind == "eng" else self.dsem[key]
        self.eng[waiter].wait_ge(sem, val)
        self.waited[k] = val

    def deps(self, e, reads, writes):
        for b in reads:
            if b.w is not None:
                self._wait(e, b.w)
        for b in writes:
            if b.w is not None:
                self._wait(e, b.w)
            for ev in b.r.values():
                self._wait(e, ev)

    def op(self, e, fn, reads=(), writes=()):
        self.deps(e, reads, writes)
        ins = fn()
        ins.then_inc(self.sem[e], 1)
        self.cnt[e] += 1
        ev = ("eng", e, self.cnt[e])
        for b in reads:
            b.r[e] = ev
        for b in writes:
            b.w = ev
            b.r = {}
        return ins

    def dma(self, q, fn, reads=(), writes=(), key=None):
        self.deps(q, reads, writes)
        if key is None:
            key = (writes[0].name if writes else reads[0].name)
        if key not in self.dsem:
            self.dsem[key] = self.es.enter_context(self.nc.semaphore("dsem%d" % self.nd))
            self.nd += 1
            self.dcnt[key] = 0
        ins = fn()
        ins.then_inc(self.dsem[key], 16)
        self.dcnt[key] += 16
        ev = ("dma", key, self.dcnt[key])
        for b in reads:
            b.r["dma:" + str(key)] = ev
        for b in writes:
            b.w = ev
            b.r = {}
        return ins

    def finish(self, bufs):
        for b in bufs:
            if b.w is not None:
                self._wait("sp", b.w)
            for ev in b.r.values():
                self._wait("sp", ev)


class Ctx:
    def __init__(self, nc, es):
        self.nc = nc
        self.es = es
        self.n = 0

    def sb(self, shape, dt, name=None):
        self.n += 1
        return self.es.enter_context(self.nc.sbuf_tensor(name or ("sb%d" % self.n), list(shape), dt))

    def ps(self, shape, dt, name=None):
        self.n += 1
        return self.es.enter_context(self.nc.psum_tensor(name or ("ps%d" % self.n), list(shape), dt))


def own_groups(i, G):
    ns = G // 4
    return [4 * k + (i if k % 2 == 0 else 3 - i) for k in range(ns)]


def make_maskwin(i):
    out = np.zeros((2, 128, 16, 512), np.float32)
    p = np.arange(128)[:, None]
    c = np.arange(512)[None, :]
    for w, delta in enumerate([i, 3 - i]):
        for j in range(16):
            r = j - 4 * delta
            if r < 0:
                continue
            vis = (128 * r + p) < c
            out[w, :, j, :] = np.where(vis, 0.0, NEG)
    return out.reshape(2, 128, 16 * 512).astype(ml_dtypes.bfloat16)


DBG = False


def build_main(G):
    SEQ = 512 * G
    NS = G // 4
    NB = SEQ // 128
    TOWN = NS * 512
    nc = bass.Bass("TRN2", target_bir_lowering=False)
    din = lambda n, s, d=F32: nc.dram_tensor(n, list(s), d, kind="ExternalInput").ap()
    dout = lambda n, s, d=F32: nc.dram_tensor(n, list(s), d, kind="ExternalOutput").ap()
    dscr = lambda n, s, d=BF16: nc.dram_tensor(n, list(s), d, kind="Internal").ap()
    xb = din("xb", [SEQ, D]); xo = din("xo", [TOWN, D]); xh = din("xh", [8, D])
    maskwin = din("maskwin", [2, 128, 8192], BF16)
    ident_d = din("ident", [128, 128], BF16); tri_d = din("tri", [128, 128], BF16); fix_d = din("fixm", [128, 128], BF16)
    identf_d = din("identf", [128, 128], F32)
    mem = din("mem", [NMEM, D])
    g_preT = din("g_mix_preT", [128, 8]); w_in = din("w_in", [D, INC]); b_sb = din("b_sb", [1, NH])
    wconvT = din("wconvT", [128, 4, 3]); g_memT = din("g_memT", [128, 8]); w_mem = din("w_mem_kv", [D, 1024])
    w_gate = din("w_gate", [D, 3 * D]); b_gateT = din("b_gateT", [128, 24])
    w_sbo = din("w_sb_o", [SBW, D]); w_cvo = din("w_conv_o", [CW, D]); w_xao = din("w_xa_o", [512, D])
    w_o = din("w_o", [D, D]); g_post = din("g_mix_post", [1, D]); g_fpreT = din("g_ffn_preT", [128, 8])
    w_up = din("w_up", [D, DFF]); w_down = din("w_down", [DFF, D]); g_fpost = din("g_ffn_post", [1, D])
    xs4 = din("xs4", [4, D]); ysbT_s = din("ysbT_s", [128, 4, 4]); stT_d = din("stT", [128, 4, 2, 4])
    cmk = din("cmk", [4, NMEM, 512]); cmv = din("cmv", [4, NMEM, 512])
    ys_o = dout("ys_o", [4, D]); ks_o = dout("ks_o", [4, SBW]); vs_o = dout("vs_o", [4, SBW]); cs_o = dout("cs_o", [128, 4, 2, 4])
    y_o = dout("y_o", [TOWN, D]); k_o = dout("k_o", [TOWN, SBW]); v_o = dout("v_o", [TOWN, SBW])
    conv_o = dout("conv_o", [NS, 128, 4, 2]); mk_o = dout("mk_o", [NMEM, 512]); mv_o = dout("mv_o", [NMEM, 512])
    KTs = dscr("KTs", [4, 128, SEQ]); Vs = dscr("Vs", [4, 128, NB, 128])
    YTs = dscr("YTs", [128, 4, TOWN])

    es = contextlib.ExitStack()
    with es:
        tr = Tracker(nc, es)
        cx = Ctx(nc, es)
        out_bufs = []

        def barrier():
            for e in ["pe", "act", "dve", "pool"]:
                for s in ["pe", "act", "dve", "pool"]:
                    if s != e and tr.cnt[s] > 0:
                        tr._wait(e, ("eng", s, tr.cnt[s]))
            for k, v in tr.dcnt.items():
                for e in ["pe", "act", "dve", "pool", "sp"]:
                    tr._wait(e, ("dma", k, v))

        ident = cx.sb([128, 128], BF16); tri = cx.sb([128, 128], BF16); fixm = cx.sb([128, 128], BF16)
        identf = cx.sb([128, 128], F32)
        gpreT = cx.sb([128, 8], F32); gfpreT = cx.sb([128, 8], F32); gmemT = cx.sb([128, 8], F32)
        gpost_b = cx.sb([128, D], F32); gfpost_b = cx.sb([128, D], F32)
        bsb_b = cx.sb([128, NH], F32); wcv = cx.sb([128, 4, 3], F32); bgt = cx.sb([128, 24], F32)
        B_const = Buf("const")
        ld = lambda o, i: tr.dma("sp", lambda: nc.sync.dma_start(out=o, in_=i), writes=[B_const], key="const")
        ld(ident[:], ident_d); ld(tri[:], tri_d); ld(fixm[:], fix_d); ld(identf[:], identf_d)
        ld(gpreT[:], g_preT); ld(gfpreT[:], g_fpreT); ld(gmemT[:], g_memT)
        ld(gpost_b[:], g_post.partition_broadcast(128)); ld(gfpost_b[:], g_fpost.partition_broadcast(128))
        ld(bsb_b[:], b_sb.partition_broadcast(128))
        ld(wcv[:], wconvT); ld(bgt[:], b_gateT)
        ebs = cx.sb([128, NH], F32); B_ebs = Buf("ebs")
        tr.op("act", lambda: nc.scalar.activation(out=ebs[:], in_=bsb_b[:], func=AF.Exp), reads=[B_const], writes=[B_ebs])

        bigs = [cx.ps([128, 1024], F32, name="big%d" % i) for i in range(4)]
        banks = [bigs[i // 2][:, (i % 2) * 512:(i % 2 + 1) * 512] for i in range(8)]
        Bbank = [Buf("bank%d" % i) for i in range(8)]

        NXT = 3
        xt = [cx.sb([128, D], F32) for _ in range(NXT)]; B_xt = [Buf("xt%d" % i) for i in range(NXT)]
        ss = cx.sb([128, 8], F32); B_ss = [Buf("ss%d" % i) for i in range(4)]
        xn = [cx.sb([128, D], BF16) for _ in range(2)]; B_xn = [Buf("xn0"), Buf("xn1")]
        xnT = [cx.sb([128, 8, 512], BF16) for _ in range(2)]; B_xnT = [Buf("xnT0"), Buf("xnT1")]
        state = {"tile": 0, "bank": 0, "ev": 0, "ft": 0}

        def nextbank():
            b = state["bank"] % 8
            state["bank"] += 1
            return banks[b], Bbank[b]

        def evac(out_ap, in_ap, reads, writes):
            state["ev"] += 1
            if state["ev"] % 2 == 0:
                tr.op("act", lambda: nc.scalar.copy(out=out_ap, in_=in_ap), reads=reads, writes=writes)
            else:
                tr.op("dve", lambda: nc.vector.tensor_copy(out=out_ap, in_=in_ap), reads=reads, writes=writes)

        def rstd_of(X, BX, nrows, i):
            BS = B_ss[i % 4]
            sc = ss[:, 2 * (i % 4):2 * (i % 4) + 1]; sc2 = ss[:, 2 * (i % 4) + 1:2 * (i % 4) + 2]
            XN = xn[i % 2]; BXN = B_xn[i % 2]
            tr.op("act", lambda: nc.scalar.activation(out=XN[0:nrows, :], in_=X, func=AF.Square,
                                                      accum_out=sc[0:nrows, :]), reads=(BX if isinstance(BX, list) else [BX]), writes=[BXN, BS])
            tr.op("act", lambda: nc.scalar.activation(out=sc2[0:nrows, :], in_=sc[0:nrows, :], func=AF.Ln,
                                                      bias=float(EPS), scale=1.0 / D), reads=[BS], writes=[BS])
            tr.op("act", lambda: nc.scalar.activation(out=sc[0:nrows, :], in_=sc2[0:nrows, :], func=AF.Exp, scale=-0.5),
                  reads=[BS], writes=[BS])
            return sc[0:nrows, :], BS

        def norm_T(X, BX, nrows, xT, B_xT, col0, gT):
            i = state["tile"]; state["tile"] += 1
            XN = xn[i % 2]; BXN = B_xn[i % 2]
            sc, BS = rstd_of(X, BX, nrows, i)
            tr.op("dve", lambda: nc.vector.tensor_scalar(out=XN[0:nrows, :], in0=X, scalar1=sc, scalar2=None, op0=ALU.mult),
                  reads=(BX if isinstance(BX, list) else [BX]) + [BS], writes=[BXN])
            bk, Bbk = nextbank()
            bkb = bk[:].bitcast(BF16)
            for c in range(8):
                tr.op("pe", lambda c=c: nc.tensor.transpose(out=bkb[:, c * 128:c * 128 + nrows],
                                                            in_=XN[0:nrows, c * 128:(c + 1) * 128], identity=ident[0:nrows, 0:nrows]),
                      reads=[BXN, B_const], writes=[Bbk])
            tr.op("dve", lambda: nc.vector.tensor_tensor(
                out=xT[:, :, col0:col0 + nrows], in0=bkb.rearrange("p (c t) -> p c t", t=128)[:, :, 0:nrows],
                in1=gT[:, :].unsqueeze(2).to_broadcast([128, 8, nrows]), op=ALU.mult), reads=[Bbk, B_const], writes=[B_xT])

        def front_tile(src_rows, nrows, xT, B_xT, col0, gT):
            j = state["ft"] % NXT; state["ft"] += 1
            tr.dma("sp", lambda: nc.sync.dma_start(out=xt[j][0:nrows, :], in_=src_rows), writes=[B_xt[j]])
            norm_T(xt[j][0:nrows, :], B_xt[j], nrows, xT, B_xT, col0, gT)

        def front_group(src, xT, B_xT, gT):
            for t in range(4):
                front_tile(src[t * 128:(t + 1) * 128, :], 128, xT, B_xT, t * 128, gT)

        es_qa = contextlib.ExitStack()
        es_qa.__enter__()
        cqa = Ctx(nc, es_qa); cqa.n = 500
        QT = cqa.sb([128, 4, TOWN], BF16); B_QT = Buf("QT")
        YT = cqa.sb([128, 4, TOWN], BF16); B_YT = [Buf("YT%d" % k) for k in range(NS)]

        wv = lambda w: w.rearrange("(kc p) n -> p kc n", p=128)
        chunks = []
        for j in range(4):
            chunks.append(wv(w_in)[:, :, 1536 + 512 * j:1536 + 512 * (j + 1)])
        for half in range(2):
            for br in range(3):
                chunks.append(wv(w_gate)[:, :, br * D + half * 512: br * D + (half + 1) * 512])
        for half in range(2):
            for w in (w_sbo, w_cvo, w_xao):
                chunks.append(wv(w)[:, :, half * 512:(half + 1) * 512])
        for half in range(2):
            chunks.append(wv(w_o)[:, :, half * 512:(half + 1) * 512])
        for j in range(8):
            chunks.append(wv(w_up)[:, :, 512 * j:512 * (j + 1)])
        for half in range(2):
            for kg in range(4):
                chunks.append(wv(w_down)[:, 8 * kg:8 * kg + 8, half * 512:(half + 1) * 512])
        chunks.append(wv(w_in)[:, :, 512:1024]); chunks.append(wv(w_in)[:, :, 1024:1536])
        assert len(chunks) == 36
        wsc2 = dscr("wsc2", [36, 128, 4096])

        with contextlib.ExitStack() as es1:
            c1 = Ctx(nc, es1)
            c1.n = 1000
            wqkv = c1.sb([128, 8, 1536], BF16); B_wqkv = Buf("wqkv")
            for j in range(3):
                tr.dma("pool", lambda j=j: nc.gpsimd.dma_start(out=wqkv[:, :, 512 * j:512 * (j + 1)],
                                                               in_=wv(w_in)[:, :, 512 * j:512 * (j + 1)]),
                       writes=[B_wqkv], key="wqkv%d" % j)
            KTst = [c1.sb([128, 4, 512], BF16) for _ in range(2)]; B_KTst = [Buf("KTst0"), Buf("KTst1")]
            Vst = [c1.sb([128, 4, 512], BF16) for _ in range(2)]; B_Vst = [Buf("Vst0"), Buf("Vst1")]
            kvst = [c1.sb([128, 512], F32) for _ in range(2)]; B_kvst = [Buf("kvst0"), Buf("kvst1")]

            for g in range(G):
                gb = g % 2
                front_group(xb[g * 512:(g + 1) * 512, :], xnT[gb], B_xnT[gb], gpreT)
                for p in range(4):
                    bk, Bbk = nextbank()
                    for c in range(8):
                        tr.op("pe", lambda c=c, p=p, bk=bk: nc.tensor.matmul(bk[:], lhsT=wqkv[:, c, 512 + p * 128:512 + (p + 1) * 128],
                                                                             rhs=xnT[gb][:, c, :], start=(c == 0), stop=(c == 7)),
                              reads=[B_wqkv, B_xnT[gb]], writes=[Bbk])
                    evac(KTst[gb][:, p, :], bk[:], [Bbk], [B_KTst[gb]])
                for t in range(4):
                    bk, Bbk = nextbank()
                    for c in range(8):
                        tr.op("pe", lambda c=c, t=t, bk=bk: nc.tensor.matmul(bk[:], lhsT=xnT[gb][:, c, t * 128:(t + 1) * 128],
                                                                             rhs=wqkv[:, c, 1024:1536], start=(c == 0), stop=(c == 7)),
                              reads=[B_wqkv, B_xnT[gb]], writes=[Bbk])
                    evac(Vst[gb][:, t, :], bk[:], [Bbk], [B_Vst[gb]])
                tr.dma("pool", lambda g=g, gb=gb: nc.gpsimd.dma_start(out=KTs[:, :, g * 512:(g + 1) * 512].rearrange("q p n -> p q n"),
                                                                      in_=KTst[gb][:]), reads=[B_KTst[gb]], key="kts%d" % gb)
                for q in range(4):
                    tr.dma("pool", lambda g=g, gb=gb, q=q: nc.gpsimd.dma_start(
                        out=Vs[q, :, 4 * g:4 * g + 4, :], in_=Vst[gb][:, :, q * 128:(q + 1) * 128]),
                        reads=[B_Vst[gb]], key="vs%d_%d" % (gb, q))

            for k in range(NS):
                gb = k % 2
                front_group(xo[k * 512:(k + 1) * 512, :], xnT[gb], B_xnT[gb], gpreT)
                for p in range(4):
                    bk, Bbk = nextbank()
                    for c in range(8):
                        tr.op("pe", lambda c=c, p=p, bk=bk: nc.tensor.matmul(bk[:], lhsT=wqkv[:, c, p * 128:(p + 1) * 128],
                                                                             rhs=xnT[gb][:, c, :], start=(c == 0), stop=(c == 7)),
                              reads=[B_wqkv, B_xnT[gb]], writes=[Bbk])
                    evac(QT[:, p, k * 512:(k + 1) * 512], bk[:], [Bbk], [B_QT])
                for t in range(4):
                    for which, dst in ((0, k_o), (1, v_o)):
                        bk, Bbk = nextbank()
                        for c in range(8):
                            tr.op("pe", lambda c=c, t=t, bk=bk, which=which: nc.tensor.matmul(
                                bk[:], lhsT=xnT[gb][:, c, t * 128:(t + 1) * 128],
                                rhs=wqkv[:, c, 512 + 512 * which:1024 + 512 * which], start=(c == 0), stop=(c == 7)),
                                reads=[B_wqkv, B_xnT[gb]], writes=[Bbk])
                        sb_i = (2 * t + which) % 2
                        evac(kvst[sb_i][:], bk[:], [Bbk], [B_kvst[sb_i]])
                        B_o = Buf("kvo"); out_bufs.append(B_o)
                        r0 = k * 512 + t * 128
                        tr.dma("pool", lambda dst=dst, r0=r0, sb_i=sb_i: nc.gpsimd.dma_start(out=dst[r0:r0 + 128, :], in_=kvst[sb_i][:]),
                               reads=[B_kvst[sb_i]], writes=[B_o], key="kvst%d" % sb_i)
            barrier()
        with contextlib.ExitStack() as es2:
            c2 = Ctx(nc, es2)
            c2.n = 2000
            mw = c2.sb([128, 2, 8192], BF16); B_mw = Buf("mw")
            wtmp = c2.sb([128, 4096], BF16); B_wtmp = Buf("wtmp")
            for j, ch in enumerate(chunks):
                kc = ch.shape[1]
                tr.dma("pool", lambda ch=ch, kc=kc: nc.gpsimd.dma_start(
                    out=wtmp[:, 0:kc * 512].rearrange("p (k n) -> p k n", n=512), in_=ch), writes=[B_wtmp], key="wtl")
                tr.dma("pool", lambda j=j: nc.gpsimd.dma_start(out=wsc2[j], in_=wtmp[:]), reads=[B_wtmp], key="wts")
            tr.dma("sp", lambda: nc.sync.dma_start(out=mw[:], in_=maskwin.rearrange("w p n -> p w n")), writes=[B_mw])
            KT = [c2.sb([128, SEQ], BF16) for _ in range(2)]; B_KT = [Buf("KT0"), Buf("KT1")]
            VV = [c2.sb([128, NB, 128], BF16) for _ in range(2)]; B_VV = [Buf("VV0"), Buf("VV1")]
            Zb = [[banks[0], banks[1]], [banks[2], banks[3]]]; B_Zb = [[Bbank[0], Bbank[1]], [Bbank[2], Bbank[3]]]
            NE, NL, NG, NW = 4, 3, 2, 2
            Eb = [c2.sb([128, 2, 512], BF16) for _ in range(NE)]; B_E = [[Buf("E%d_%d" % (i, h)) for h in range(2)] for i in range(NE)]
            Lb = [c2.sb([128, 2, 512], BF16) for _ in range(NL)]; B_L = [Buf("L%d" % i) for i in range(NL)]
            Gb = [c2.sb([128, 2, 512], BF16) for _ in range(NG)]; B_G = [Buf("G%d" % i) for i in range(NG)]
            Wb = [c2.sb([128, 2, 512], BF16) for _ in range(NW)]; B_W = [Buf("W%d" % i) for i in range(NW)]
            for p in range(4):
                tr.dma("sp", lambda p=p: nc.sync.dma_start(out=KT[p % 2][:], in_=KTs[p]), writes=[B_KT[p % 2]])
                tr.dma("sp", lambda p=p: nc.sync.dma_start(out=VV[p % 2][:], in_=Vs[p]), writes=[B_VV[p % 2]])
                steps = [(k, kb) for k in range(NS) for kb in range(16 * (k + 1) - 1, -1, -1)]
                NSTEP = len(steps)
                KTp = KT[p % 2]; VVp = VV[p % 2]; BKT = B_KT[p % 2]; BVV = B_VV[p % 2]

                def emit_Z(n):
                    k, kb = steps[n]
                    masked = kb >= 16 * k
                    for h in range(2):
                        ph = slice(64 * h, 64 * h + 64)
                        zb = Zb[n % 2][h]
                        tr.op("pe", lambda zb=zb, ph=ph: nc.tensor.matmul(zb[:], lhsT=KTp[ph, kb * 128:(kb + 1) * 128],
                                                                          rhs=QT[ph, p, k * 512:(k + 1) * 512], start=True, stop=True),
                              reads=[BKT, B_QT], writes=[B_Zb[n % 2][h]])
                    if masked:
                        j = kb - 16 * k
                        for h in range(2):
                            zb = Zb[n % 2][h]
                            tr.op("pe", lambda zb=zb, j=j: nc.tensor.matmul(zb[:], lhsT=ident[:, :], rhs=mw[:, k % 2, j * 512:(j + 1) * 512],
                                                                            start=False, stop=True),
                                  reads=[B_mw, B_const], writes=[B_Zb[n % 2][h]])

                def emit_E(n):
                    tr.op("act", lambda: nc.scalar.activation(out=Eb[n % NE][:].rearrange("p h n -> p (h n)"), in_=bigs[n % 2][:], func=AF.Exp,
                                                              scale=0.125),
                          reads=[B_Zb[n % 2][0], B_Zb[n % 2][1]], writes=[B_E[n % NE][0], B_E[n % NE][1]])
                    for h in range(2):
                        hd = 2 * p + h
                        tr.op("dve", lambda h=h, hd=hd: nc.vector.tensor_scalar(out=Eb[n % NE][:, h, :], in0=Eb[n % NE][:, h, :],
                                                                                scalar1=ebs[:, hd:hd + 1], scalar2=None, op0=ALU.mult),
                              reads=[B_E[n % NE][h], B_ebs], writes=[B_E[n % NE][h]])

                def emit_L(n):
                    tr.op("act", lambda: nc.scalar.activation(out=Lb[n % NL][:], in_=Eb[n % NE][:], func=AF.Ln, bias=1.0, scale=1.0),
                          reads=B_E[n % NE], writes=[B_L[n % NL]])

                def emit_Tri(n):
                    k, kb = steps[n]
                    first = kb == 16 * (k + 1) - 1
                    for h in range(2):
                        tr.op("pe", lambda h=h: nc.tensor.matmul(banks[4 + h][:], lhsT=tri[:, :], rhs=Lb[n % NL][:, h, :], start=first, stop=True),
                              reads=[B_L[n % NL], B_const], writes=[Bbank[4 + h]])

                def emit_G(n):
                    tr.op("act", lambda: nc.scalar.activation(out=Gb[n % NG][:].rearrange("p h n -> p (h n)"), in_=bigs[2][:], func=AF.Exp, scale=-1.0),
                          reads=[Bbank[4], Bbank[5]], writes=[B_G[n % NG]])

                def emit_fix(n):
                    k, kb = steps[n]
                    if kb == 0:
                        return
                    for h in range(2):
                        tr.op("pe", lambda h=h: nc.tensor.matmul(banks[4 + h][:], lhsT=fixm[:, :], rhs=Lb[n % NL][:, h, :], start=False, stop=True),
                              reads=[B_L[n % NL], B_const], writes=[Bbank[4 + h]])

                def emit_W(n):
                    tr.op("dve", lambda: nc.vector.tensor_tensor(out=Wb[n % NW][:], in0=Eb[n % NE][:], in1=Gb[n % NG][:], op=ALU.mult),
                          reads=B_E[n % NE] + [B_G[n % NG]], writes=[B_W[n % NW]])

                def emit_V(n):
                    k, kb = steps[n]
                    first = kb == 16 * (k + 1) - 1
                    yb = 6 + (k % 2)
                    for h in range(2):
                        tr.op("pe", lambda h=h: nc.tensor.matmul(banks[yb][64 * h:64 * h + 64, :], lhsT=VVp[:, kb, 64 * h:64 * h + 64],
                                                                 rhs=Wb[n % NW][:, h, :], start=first, stop=(kb == 0)),
                              reads=[B_W[n % NW], BVV], writes=[Bbank[yb]])
                    if kb == 0:
                        tr.op("dve", lambda: nc.vector.tensor_copy(out=YT[:, p, k * 512:(k + 1) * 512], in_=banks[yb][:]),
                              reads=[Bbank[yb]], writes=[B_YT[k]])

                emit_Z(0); emit_E(0); emit_L(0)
                if NSTEP > 1:
                    emit_Z(1); emit_E(1)
                for n in range(NSTEP):
                    emit_Tri(n)
                    if n + 2 < NSTEP:
                        emit_Z(n + 2)
                    emit_G(n)
                    if n + 1 < NSTEP:
                        emit_L(n + 1)
                    if n + 2 < NSTEP:
                        emit_E(n + 2)
                    if n >= 1:
                        emit_V(n - 1)
                    emit_fix(n)
                    emit_W(n)
                emit_V(NSTEP - 1)
            barrier()
        B_YTs = Buf("YTs")
        tr.dma("sp", lambda: nc.sync.dma_start(out=YTs, in_=YT[:]), reads=B_YT, writes=[B_YTs])
        if DBG:
            ytd = dout("yt_dbg", [128, 4, TOWN], BF16)
            B_o = Buf("ytd"); out_bufs.append(B_o)
            tr.dma("sp", lambda: nc.sync.dma_start(out=ytd, in_=YT[:]), reads=B_YT, writes=[B_o])
        barrier()
        es_qa.__exit__(None, None, None)
        with contextlib.ExitStack() as es3:
            c3 = Ctx(nc, es3); c3.n = 3000
            NR = 4
            ring = [c3.sb([128, 4096], BF16) for _ in range(NR)]; B_ring = [Buf("ring%d" % i) for i in range(NR)]
            r3 = lambda t: t[:].rearrange("p (k n) -> p k n", n=512)

            def slot_order():
                o = [1, 2, 0, 3]
                for half in range(2):
                    for br in range(3):
                        o += [4 + half * 3 + br, 10 + half * 3 + br]
                o += [16, 17] + list(range(18, 26)) + list(range(26, 34))
                return o
            use_list = []
            for k in range(NS):
                use_list += slot_order()
            use_list += [34, 35] + slot_order()
            wstate = {"issued": 0, "used": 0}

            def issue_to(n):
                while wstate["issued"] < min(n, len(use_list)):
                    i = wstate["issued"]; j = use_list[i]; sl = i % NR
                    tr.dma("pool", lambda j=j, sl=sl: nc.gpsimd.dma_start(out=ring[sl][:], in_=wsc2[j]),
                           writes=[B_ring[sl]], key="ring%d" % sl)
                    wstate["issued"] += 1

            def next_chunk(expect):
                i = wstate["used"]
                assert use_list[i] == expect, (i, use_list[i], expect)
                issue_to(i + NR - 1)
                wstate["used"] += 1
                return r3(ring[i % NR]), B_ring[i % NR]

            xres = c3.sb([128, 4, D], F32); B_xres = Buf("xres")
            ysbT = c3.sb([128, 4, 512], BF16); B_ysbT = Buf("ysbT")
            ccT = c3.sb([128, 4, 512], F32); B_ccT = Buf("ccT")
            cbT = c3.sb([128, 4, 512], BF16); B_cbT = Buf("cbT")
            pext = c3.sb([128, 4, 514], F32); B_pext = Buf("pext")
            xqT = c3.sb([128, 4, 512], BF16); B_xqT = Buf("xqT")
            yconvT = c3.sb([128, 4, 512], BF16); B_ycv = Buf("ycv")
            yxaT = c3.sb([128, 4, 512], BF16); B_yxa = Buf("yxa")
            mT = c3.sb([128, 8, 512], BF16); B_mT = Buf("mT")
            fsb = c3.sb([128, 4, 512], F32); B_fsb = Buf("fsb")
            f1T = c3.sb([128, 32, 512], BF16); B_f1T = Buf("f1T")
            gsb = c3.sb([128, 512], F32); B_gsb = Buf("gsb")
            acc = c3.sb([128, 512], F32); B_acc = Buf("acc")
            sqb = c3.sb([128, 512], F32); B_sqb = Buf("sqb")
            macc = c3.sb([128, 4, 512], F32); B_macc = Buf("macc")
            mkT = c3.sb([128, 4, 256], BF16); mvb = c3.sb([128, 2, 512], BF16); B_mkv = Buf("mkv")
            Pf = c3.sb([128, 4, 256], F32); B_Pf = Buf("Pf")
            Pn = c3.sb([128, 4, 256], BF16); B_Pn = Buf("Pn")
            PT = c3.sb([128, 8, 128], BF16); B_PT = Buf("PT")
            xnTh = c3.sb([128, 8, 8], BF16); B_xnTh = Buf("xnTh")
            ccH = c3.sb([128, 4, 8], F32); pH = c3.sb([128, 4, 8], F32); B_H = Buf("halo")
            smx = c3.sb([128, 16], F32); B_smx = Buf("smx")
            XS = 1.0 / math.sqrt(XAD)

            def mm8(bk, Bbk, lhs_fn, rhs_fn, reads, n=8):
                for c in range(n):
                    tr.op("pe", lambda c=c: nc.tensor.matmul(bk, lhsT=lhs_fn(c), rhs=rhs_fn(c), start=(c == 0), stop=(c == n - 1)),
                          reads=reads, writes=[Bbk])

            def pairbank():
                if state["bank"] % 2:
                    state["bank"] += 1
                b = state["bank"] % 8
                state["bank"] += 2
                return bigs[b // 2][:], [Bbank[b], Bbank[b + 1]]

            tr.dma("pool", lambda: nc.gpsimd.dma_start(out=r3(ring[0]), in_=wv(w_mem)[:, :, 0:512]), writes=[B_ring[0]], key="ring0")
            tr.dma("pool", lambda: nc.gpsimd.dma_start(out=r3(ring[1]), in_=wv(w_mem)[:, :, 512:1024]), writes=[B_ring[1]], key="ring1")
            wmk = r3(ring[0]); wmv = r3(ring[1])
            memT = xnT[0]
            for mt in range(2):
                front_tile(mem[mt * 128:(mt + 1) * 128, :], 128, memT, B_xnT[0], mt * 128, gmemT)
            for h in range(4):
                bk, Bbk = nextbank()
                mm8(bk[:, 0:256], Bbk, lambda c, h=h: wmk[:, c, h * 128:(h + 1) * 128], lambda c: memT[:, c, 0:256], [B_ring[0], B_xnT[0]])
                evac(mkT[:, h, :], bk[:, 0:256], [Bbk], [B_mkv])
            for mt in range(2):
                for which, wsrc, dst in ((0, wmk, mk_o), (1, wmv, mv_o)):
                    bk, Bbk = nextbank()
                    mm8(bk, Bbk, lambda c, mt=mt: memT[:, c, mt * 128:(mt + 1) * 128], lambda c, wsrc=wsrc: wsrc[:, c, :],
                        [B_ring[which], B_xnT[0]])
                    evac(fsb[:, 2 * mt + which, :], bk, [Bbk], [B_fsb])
                    if which == 1:
                        evac(mvb[:, mt, :], bk, [Bbk], [B_mkv])
                    B_o = Buf("mo"); out_bufs.append(B_o)
                    tr.dma("sp", lambda dst=dst, mt=mt, which=which: nc.sync.dma_start(out=dst[mt * 128:(mt + 1) * 128, :],
                                                                                      in_=fsb[:, 2 * mt + which, :]),
                           reads=[B_fsb], writes=[B_o], key="mo%d%d" % (mt, which))
            front_tile(xh[0:8, :], 8, xnTh, B_xnTh, 0, gpreT)

            for k in range(NS):
                X0 = xnT[0]; BX0 = B_xnT[0]; HN = xnT[1]; BHN = B_xnT[1]
                front_group(xo[k * 512:(k + 1) * 512, :], X0, BX0, gpreT)
                tr.dma("sp", lambda k=k: nc.sync.dma_start(out=xres[:], in_=xo[k * 512:(k + 1) * 512, :].rearrange("(t p) d -> p t d", p=128)),
                       writes=[B_xres])
                tr.dma("sp", lambda k=k: nc.sync.dma_start(out=ysbT[:], in_=YTs[:, :, k * 512:(k + 1) * 512]), reads=[B_YTs], writes=[B_ysbT])
                ch3, Bch = next_chunk(1)
                for oc in range(4):
                    bk, Bbk = nextbank()
                    mm8(bk, Bbk, lambda c, oc=oc: ch3[:, c, oc * 128:(oc + 1) * 128], lambda c: X0[:, c, :], [Bch, BX0])
                    evac(ccT[:, oc, :], bk, [Bbk], [B_ccT])
                    if k == 0:
                        bk, Bbk = nextbank()
                        mm8(bk[:, 0:8], Bbk, lambda c, oc=oc: ch3[:, c, oc * 128:(oc + 1) * 128], lambda c: xnTh[:, c, :], [Bch, B_xnTh])
                        evac(ccH[:, oc, :], bk[:, 0:8], [Bbk], [B_H])
                ch3, Bch = next_chunk(2)
                for oc in range(4):
                    bk, Bbk = nextbank()
                    mm8(bk, Bbk, lambda c, oc=oc: ch3[:, c, oc * 128:(oc + 1) * 128], lambda c: X0[:, c, :], [Bch, BX0])
                    tr.op("dve", lambda oc=oc, bk=bk: nc.vector.tensor_tensor(out=pext[:, oc, 2:514], in0=ccT[:, oc, :], in1=bk, op=ALU.mult),
                          reads=[Bbk, B_ccT], writes=[B_pext])
                    if k == 0:
                        bk, Bbk = nextbank()
                        mm8(bk[:, 0:8], Bbk, lambda c, oc=oc: ch3[:, c, oc * 128:(oc + 1) * 128], lambda c: xnTh[:, c, :], [Bch, B_xnTh])
                        tr.op("dve", lambda oc=oc, bk=bk: nc.vector.tensor_tensor(out=pH[:, oc, :], in0=ccH[:, oc, :], in1=bk[:, 0:8], op=ALU.mult),
                              reads=[Bbk, B_H], writes=[B_H])
                tr.op("dve", lambda k=k: nc.vector.tensor_copy(out=pext[:, :, 0:2], in_=pH[:, :, 2 * k:2 * k + 2]), reads=[B_H], writes=[B_pext])
                ch3, Bch = next_chunk(0)
                for oc in range(4):
                    bk, Bbk = nextbank()
                    mm8(bk, Bbk, lambda c, oc=oc: ch3[:, c, oc * 128:(oc + 1) * 128], lambda c: X0[:, c, :], [Bch, BX0])
                    evac(cbT[:, oc, :], bk, [Bbk], [B_cbT])
                ch3, Bch = next_chunk(3)
                for oc in range(4):
                    bk, Bbk = nextbank()
                    mm8(bk, Bbk, lambda c, oc=oc: ch3[:, c, oc * 128:(oc + 1) * 128], lambda c: X0[:, c, :], [Bch, BX0])
                    evac(xqT[:, oc, :], bk, [Bbk], [B_xqT])
                for oc in range(4):
                    tr.op("dve", lambda oc=oc: nc.vector.tensor_scalar(out=acc[:], in0=pext[:, oc, 0:512], scalar1=wcv[:, oc, 0:1], scalar2=None,
                                                                       op0=ALU.mult), reads=[B_pext, B_const], writes=[B_acc])
                    for i in (1, 2):
                        tr.op("dve", lambda oc=oc, i=i: nc.vector.scalar_tensor_tensor(out=acc[:], in0=pext[:, oc, i:i + 512], scalar=wcv[:, oc, i:i + 1],
                                                                                       in1=acc[:], op0=ALU.mult, op1=ALU.add),
                              reads=[B_pext, B_const, B_acc], writes=[B_acc])
                    tr.op("dve", lambda oc=oc: nc.vector.tensor_tensor(out=yconvT[:, oc, :], in0=acc[:], in1=cbT[:, oc, :], op=ALU.mult),
                          reads=[B_acc, B_cbT], writes=[B_ycv])
                B_o = Buf("cvo"); out_bufs.append(B_o)
                tr.dma("sp", lambda k=k: nc.sync.dma_start(out=conv_o[k], in_=pext[:, :, 512:514]), reads=[B_pext], writes=[B_o], key="cvo")
                for t in range(4):
                    pb, Bpb = pairbank()
                    for h in range(4):
                        tr.op("pe", lambda h=h, t=t, pb=pb: nc.tensor.matmul(pb[:, h * 256:(h + 1) * 256], lhsT=xqT[:, h, t * 128:(t + 1) * 128],
                                                                             rhs=mkT[:, h, :], start=True, stop=True),
                              reads=[B_xqT, B_mkv], writes=[Bpb[h // 2]])
                    tr.op("dve", lambda pb=pb: nc.vector.tensor_reduce(out=smx[:, 0:4], in_=pb.rearrange("p (h m) -> p h m", m=256), axis=AX.X, op=ALU.max),
                          reads=Bpb, writes=[B_smx])
                    tr.op("dve", lambda: nc.vector.tensor_scalar(out=smx[:, 4:8], in0=smx[:, 0:4], scalar1=-XS, scalar2=None, op0=ALU.mult),
                          reads=[B_smx], writes=[B_smx])
                    for h in range(4):
                        tr.op("act", lambda h=h, pb=pb: nc.scalar.activation(out=Pf[:, h, :], in_=pb[:, h * 256:(h + 1) * 256], func=AF.Exp,
                                                                             bias=smx[:, 4 + h:5 + h], scale=XS, accum_out=smx[:, 8 + h:9 + h]),
                              reads=Bpb + [B_smx], writes=[B_Pf, B_smx])
                    tr.op("dve", lambda: nc.vector.reciprocal(out=smx[:, 12:16], in_=smx[:, 8:12]), reads=[B_smx], writes=[B_smx])
                    tr.op("dve", lambda: nc.vector.tensor_tensor(out=Pn[:], in0=Pf[:], in1=smx[:, 12:16].unsqueeze(2).to_broadcast([128, 4, 256]),
                                                                 op=ALU.mult), reads=[B_Pf, B_smx], writes=[B_Pn])
                    bk, Bbk = nextbank()
                    bkb = bk.bitcast(BF16)
                    for h in range(4):
                        for mc in range(2):
                            tr.op("pe", lambda h=h, mc=mc, bkb=bkb: nc.tensor.transpose(out=bkb[:, (2 * h + mc) * 128:(2 * h + mc + 1) * 128],
                                                                                        in_=Pn[:, h, mc * 128:(mc + 1) * 128], identity=ident[:, :]),
                                  reads=[B_Pn, B_const], writes=[Bbk])
                    evac(PT[:], bkb.rearrange("p (j t) -> p j t", t=128), [Bbk], [B_PT])
                    bk, Bbk = nextbank()
                    for h in range(4):
                        for mc in range(2):
                            tr.op("pe", lambda h=h, mc=mc, bk=bk: nc.tensor.matmul(bk[:, h * 128:(h + 1) * 128], lhsT=mvb[:, mc, h * 128:(h + 1) * 128],
                                                                                   rhs=PT[:, 2 * h + mc, :], start=(mc == 0), stop=(mc == 1)),
                                  reads=[B_PT, B_mkv], writes=[Bbk])
                    evac(yxaT[:, :, t * 128:(t + 1) * 128], bk.rearrange("p (h t) -> p h t", t=128), [Bbk], [B_yxa])
                ybr = [(ysbT, B_ysbT), (yconvT, B_ycv), (yxaT, B_yxa)]
                for half in range(2):
                    for br in range(3):
                        g3, Bg = next_chunk(4 + half * 3 + br)
                        b3, Bb = next_chunk(10 + half * 3 + br)
                        yb, Byb = ybr[br]
                        for ocl in range(4):
                            oc = half * 4 + ocl
                            bk, Bbk = nextbank()
                            mm8(bk, Bbk, lambda c, ocl=ocl: g3[:, c, ocl * 128:(ocl + 1) * 128], lambda c: X0[:, c, :], [Bg, BX0])
                            tr.op("act", lambda bk=bk, br=br, oc=oc: nc.scalar.activation(out=gsb[:], in_=bk, func=AF.Sigmoid,
                                                                                         bias=bgt[:, br * 8 + oc:br * 8 + oc + 1], scale=1.0),
                                  reads=[Bbk, B_const], writes=[B_gsb])
                            bk2, Bbk2 = nextbank()
                            mm8(bk2, Bbk2, lambda c, ocl=ocl: b3[:, c, ocl * 128:(ocl + 1) * 128], lambda c, yb=yb: yb[:, c, :], [Bb, Byb], n=4)
                            if br == 0:
                                tr.op("dve", lambda bk2=bk2, ocl=ocl: nc.vector.tensor_tensor(out=macc[:, ocl, :], in0=gsb[:], in1=bk2, op=ALU.mult),
                                      reads=[Bbk2, B_gsb], writes=[B_macc])
                            else:
                                tr.op("dve", lambda bk2=bk2: nc.vector.tensor_tensor(out=acc[:], in0=gsb[:], in1=bk2, op=ALU.mult),
                                      reads=[Bbk2, B_gsb], writes=[B_acc])
                                if br == 1:
                                    tr.op("dve", lambda ocl=ocl: nc.vector.tensor_tensor(out=macc[:, ocl, :], in0=macc[:, ocl, :], in1=acc[:], op=ALU.add),
                                          reads=[B_acc, B_macc], writes=[B_macc])
                                else:
                                    tr.op("dve", lambda ocl=ocl, oc=oc: nc.vector.tensor_tensor(out=mT[:, oc, :], in0=macc[:, ocl, :], in1=acc[:], op=ALU.add),
                                          reads=[B_acc, B_macc], writes=[B_mT])
                w0, Bw0 = next_chunk(16)
                w1, Bw1 = next_chunk(17)
                for t in range(4):
                    pb, Bpb = pairbank()
                    for half, (w3, Bw3) in enumerate(((w0, Bw0), (w1, Bw1))):
                        mm8(pb[:, half * 512:(half + 1) * 512], Bpb[half], lambda c, t=t: mT[:, c, t * 128:(t + 1) * 128],
                            lambda c, w3=w3: w3[:, c, :], [B_mT, Bw3])
                    i = state["tile"]; state["tile"] += 1
                    sc, BS = rstd_of(pb, Bpb, 128, i)
                    tr.op("dve", lambda pb=pb, sc=sc: nc.vector.scalar_tensor_tensor(out=xt[0][:], in0=pb, scalar=sc, in1=gpost_b[:],
                                                                                     op0=ALU.mult, op1=ALU.mult),
                          reads=Bpb + [BS, B_const], writes=[B_xt[0]])
                    tr.op("dve", lambda t=t: nc.vector.tensor_tensor(out=xres[:, t, :], in0=xt[0][:], in1=xres[:, t, :], op=ALU.add),
                          reads=[B_xt[0], B_xres], writes=[B_xres])
                for t in range(4):
                    norm_T(xres[:, t, :], [B_xres], 128, HN, BHN, t * 128, gfpreT)
                for j in range(8):
                    u3, Bu = next_chunk(18 + j)
                    for ocl in range(4):
                        fc = 4 * j + ocl
                        bk, Bbk = nextbank()
                        mm8(bk, Bbk, lambda c, ocl=ocl: u3[:, c, ocl * 128:(ocl + 1) * 128], lambda c: HN[:, c, :], [Bu, BHN])
                        tr.op("act", lambda bk=bk: nc.scalar.activation(out=sqb[:], in_=bk, func=AF.Square), reads=[Bbk], writes=[B_sqb])
                        tr.op("dve", lambda bk=bk, fc=fc: nc.vector.scalar_tensor_tensor(out=f1T[:, fc, :], in0=bk, scalar=0.0, in1=sqb[:],
                                                                                         op0=ALU.is_gt, op1=ALU.mult),
                              reads=[Bbk, B_sqb], writes=[B_f1T])
                for half in range(2):
                    b4 = [nextbank() for _ in range(4)]
                    for kg in range(4):
                        d3, Bd = next_chunk(26 + half * 4 + kg)
                        for t in range(4):
                            for kc in range(8):
                                tr.op("pe", lambda t=t, kc=kc, kg=kg: nc.tensor.matmul(b4[t][0], lhsT=f1T[:, kg * 8 + kc, t * 128:(t + 1) * 128],
                                                                                       rhs=d3[:, kc, :], start=(kg == 0 and kc == 0),
                                                                                       stop=(kg == 3 and kc == 7)),
                                      reads=[B_f1T, Bd], writes=[b4[t][1]])
                    if half == 0:
                        for t in range(4):
                            evac(fsb[:, t, :], b4[t][0], [b4[t][1]], [B_fsb])
                    else:
                        for t in range(4):
                            i = state["tile"]; state["tile"] += 1
                            BS = B_ss[i % 4]
                            sa = ss[:, 2 * (i % 4):2 * (i % 4) + 1]; sb2 = ss[:, 2 * (i % 4) + 1:2 * (i % 4) + 2]
                            XN = xn[i % 2]; BXN = B_xn[i % 2]
                            tr.op("act", lambda t=t: nc.scalar.activation(out=XN[:, 0:512], in_=fsb[:, t, :], func=AF.Square, accum_out=sa),
                                  reads=[B_fsb], writes=[BXN, BS])
                            tr.op("act", lambda t=t: nc.scalar.activation(out=XN[:, 512:1024], in_=b4[t][0], func=AF.Square, accum_out=sb2),
                                  reads=[b4[t][1]], writes=[BXN, BS])
                            tr.op("dve", lambda: nc.vector.tensor_tensor(out=sa, in0=sa, in1=sb2, op=ALU.add), reads=[BS], writes=[BS])
                            tr.op("act", lambda: nc.scalar.activation(out=sb2, in_=sa, func=AF.Ln, bias=float(EPS), scale=1.0 / D), reads=[BS], writes=[BS])
                            tr.op("act", lambda: nc.scalar.activation(out=sa, in_=sb2, func=AF.Exp, scale=-0.5), reads=[BS], writes=[BS])
                            tr.op("dve", lambda t=t: nc.vector.scalar_tensor_tensor(out=xt[0][:, 0:512], in0=fsb[:, t, :], scalar=sa, in1=gfpost_b[:, 0:512],
                                                                                    op0=ALU.mult, op1=ALU.mult), reads=[B_fsb, BS, B_const], writes=[B_xt[0]])
                            tr.op("dve", lambda t=t: nc.vector.scalar_tensor_tensor(out=xt[0][:, 512:1024], in0=b4[t][0], scalar=sa, in1=gfpost_b[:, 512:1024],
                                                                                    op0=ALU.mult, op1=ALU.mult), reads=[b4[t][1], BS, B_const], writes=[B_xt[0]])
                            tr.op("dve", lambda t=t: nc.vector.tensor_tensor(out=xt[0][:], in0=xt[0][:], in1=xres[:, t, :], op=ALU.add),
                                  reads=[B_xt[0], B_xres], writes=[B_xt[0]])
                            B_o = Buf("yo"); out_bufs.append(B_o)
                            r0 = k * 512 + t * 128
                            tr.dma("sp", lambda r0=r0: nc.sync.dma_start(out=y_o[r0:r0 + 128, :], in_=xt[0][:]), reads=[B_xt[0]], writes=[B_o], key="yo")
            if True:
                X0 = xnT[0]; BX0 = B_xnT[0]; HN = xnT[1]; BHN = B_xnT[1]
                NT = 4
                stT = c3.sb([128, 4, 2, 4], F32); B_stT = Buf("stT")
                onesf = c3.sb([4, 128], F32); B_ones = Buf("ones4")
                xq_tok = gsb[0:4, :]; Qdiag = macc[0:4, :, :]; xqrep = fsb; Mk = ccT[:, 0:2, :]
                B_xqtok = B_gsb; B_Qd = B_macc; B_xq = B_fsb; B_Mk = B_ccT
                Sx = c3.sb([128, 2, 4, 4], F32); B_Sx = Buf("Sx")
                PnT = c3.sb([128, 2, 16], F32); B_PnT = Buf("PnT")
                front_tile(xs4[0:4, :], 4, X0, BX0, 0, gpreT)
                tr.dma("sp", lambda: nc.sync.dma_start(out=xres[0:4, 0, :], in_=xs4[0:4, :]), writes=[B_xres])
                tr.dma("pool", lambda: nc.gpsimd.dma_start(out=ysbT[:, :, 0:4], in_=ysbT_s), writes=[B_ysbT])
                tr.dma("sp", lambda: nc.sync.dma_start(out=stT[:], in_=stT_d), writes=[B_stT])
                tr.op("dve", lambda: nc.vector.memset(onesf[:], 1.0), writes=[B_ones])
                for which, dst in ((0, ks_o), (1, vs_o)):
                    ch3, Bch = next_chunk(34 + which)
                    bk, Bbk = nextbank()
                    mm8(bk[0:4, :], Bbk, lambda c: X0[:, c, 0:4], lambda c: ch3[:, c, :], [Bch, BX0])
                    evac(fsb[0:4, which, :], bk[0:4, :], [Bbk], [B_fsb])
                    B_o = Buf("kso"); out_bufs.append(B_o)
                    tr.dma("sp", lambda dst=dst, which=which: nc.sync.dma_start(out=dst, in_=fsb[0:4, which, :]), reads=[B_fsb], writes=[B_o], key="kso%d" % which)
                ch3, Bch = next_chunk(1)
                for oc in range(4):
                    bk, Bbk = nextbank()
                    mm8(bk[:, 0:NT], Bbk, lambda c, oc=oc: ch3[:, c, oc * 128:(oc + 1) * 128], lambda c: X0[:, c, 0:NT], [Bch, BX0])
                    evac(ccT[:, oc, 0:NT], bk[:, 0:NT], [Bbk], [B_ccT])
                ch3, Bch = next_chunk(2)
                for oc in range(4):
                    bk, Bbk = nextbank()
                    mm8(bk[:, 0:NT], Bbk, lambda c, oc=oc: ch3[:, c, oc * 128:(oc + 1) * 128], lambda c: X0[:, c, 0:NT], [Bch, BX0])
                    tr.op("dve", lambda oc=oc, bk=bk: nc.vector.tensor_tensor(out=pext[:, oc, 0:NT], in0=ccT[:, oc, 0:NT], in1=bk[:, 0:NT], op=ALU.mult),
                          reads=[Bbk, B_ccT], writes=[B_pext])
                ch3, Bch = next_chunk(0)
                for oc in range(4):
                    bk, Bbk = nextbank()
                    mm8(bk[:, 0:NT], Bbk, lambda c, oc=oc: ch3[:, c, oc * 128:(oc + 1) * 128], lambda c: X0[:, c, 0:NT], [Bch, BX0])
                    evac(cbT[:, oc, 0:NT], bk[:, 0:NT], [Bbk], [B_cbT])
                ch3, Bch = next_chunk(3)
                for oc in range(4):
                    bk, Bbk = nextbank()
                    mm8(bk[:, 0:NT], Bbk, lambda c, oc=oc: ch3[:, c, oc * 128:(oc + 1) * 128], lambda c: X0[:, c, 0:NT], [Bch, BX0])
                    evac(xqT[:, oc, 0:NT], bk[:, 0:NT], [Bbk], [B_xqT])
                bk, Bbk = nextbank()
                mm8(bk[0:4, :], Bbk, lambda c: X0[:, c, 0:4], lambda c: ch3[:, c, :], [Bch, BX0])
                evac(xq_tok, bk[0:4, :], [Bbk], [B_xqtok])
                for oc in range(4):
                    tr.op("dve", lambda oc=oc: nc.vector.tensor_scalar(out=acc[:, 0:NT], in0=stT[:, oc, 0, :], scalar1=wcv[:, oc, 0:1], scalar2=None,
                                                                       op0=ALU.mult), reads=[B_stT, B_const], writes=[B_acc])
                    tr.op("dve", lambda oc=oc: nc.vector.scalar_tensor_tensor(out=acc[:, 0:NT], in0=stT[:, oc, 1, :], scalar=wcv[:, oc, 1:2],
                                                                              in1=acc[:, 0:NT], op0=ALU.mult, op1=ALU.add),
                          reads=[B_stT, B_const, B_acc], writes=[B_acc])
                    tr.op("dve", lambda oc=oc: nc.vector.scalar_tensor_tensor(out=acc[:, 0:NT], in0=pext[:, oc, 0:NT], scalar=wcv[:, oc, 2:3],
                                                                              in1=acc[:, 0:NT], op0=ALU.mult, op1=ALU.add),
                          reads=[B_pext, B_const, B_acc], writes=[B_acc])
                    tr.op("dve", lambda oc=oc: nc.vector.tensor_tensor(out=yconvT[:, oc, 0:NT], in0=acc[:, 0:NT], in1=cbT[:, oc, 0:NT], op=ALU.mult),
                          reads=[B_acc, B_cbT], writes=[B_ycv])
                B_o = Buf("cso"); out_bufs.append(B_o)
                tr.dma("sp", lambda: nc.sync.dma_start(out=cs_o[:, :, 0, :], in_=stT[:, :, 1, :]), reads=[B_stT], writes=[B_o], key="cso0")
                B_o = Buf("cso1"); out_bufs.append(B_o)
                tr.dma("sp", lambda: nc.sync.dma_start(out=cs_o[:, :, 1, :], in_=pext[:, :, 0:NT]), reads=[B_pext], writes=[B_o], key="cso1")
                tr.op("dve", lambda: nc.vector.tensor_tensor(out=Qdiag, in0=xq_tok.unsqueeze(1).to_broadcast([4, 4, 512]),
                                                             in1=identf[0:4, 0:4].unsqueeze(2).to_broadcast([4, 4, 512]), op=ALU.mult),
                      reads=[B_xqtok, B_const], writes=[B_Qd])
                for s in range(4):
                    bk, Bbk = nextbank()
                    tr.op("pe", lambda s=s, bk=bk: nc.tensor.matmul(bk, lhsT=onesf[0:4, :], rhs=Qdiag[:, s, :], start=True, stop=True),
                          reads=[B_Qd, B_ones], writes=[Bbk])
                    evac(xqrep[:, s, :], bk, [Bbk], [B_xq])
                for s in range(4):
                    tr.dma("sp", lambda s=s: nc.sync.dma_start(out=Mk, in_=cmk[s].rearrange("(mc p) n -> p mc n", p=128)), writes=[B_Mk])
                    tr.op("dve", lambda s=s: nc.vector.tensor_tensor(out=Mk, in0=Mk, in1=xqrep[:, s:s + 1, :].to_broadcast([128, 2, 512]), op=ALU.mult),
                          reads=[B_Mk, B_xq], writes=[B_Mk])
                    tr.op("dve", lambda s=s: nc.vector.tensor_reduce(out=Sx[:, :, s, :], in_=Mk.rearrange("p mc (h d) -> p mc h d", d=128),
                                                                     axis=AX.X, op=ALU.add), reads=[B_Mk], writes=[B_Sx])
                pb, Bpb = pairbank()
                for mc in range(2):
                    tr.op("pe", lambda mc=mc, pb=pb: nc.tensor.transpose(out=pb[0:16, mc * 128:(mc + 1) * 128],
                                                                         in_=Sx[:, mc, :, :].rearrange("p s h -> p (s h)"), identity=identf[:, :]),
                          reads=[B_Sx, B_const], writes=[Bpb[0]])
                ST = pb[0:16, 0:256]
                tr.op("dve", lambda: nc.vector.tensor_reduce(out=smx[0:16, 0:1], in_=ST, axis=AX.X, op=ALU.max), reads=[Bpb[0]], writes=[B_smx])
                tr.op("dve", lambda: nc.vector.tensor_scalar(out=smx[0:16, 4:5], in0=smx[0:16, 0:1], scalar1=-XS, scalar2=None, op0=ALU.mult),
                      reads=[B_smx], writes=[B_smx])
                tr.op("act", lambda: nc.scalar.activation(out=Pf[0:16, 0, :], in_=ST, func=AF.Exp, bias=smx[0:16, 4:5], scale=XS,
                                                          accum_out=smx[0:16, 8:9]), reads=[Bpb[0], B_smx], writes=[B_Pf, B_smx])
                tr.op("dve", lambda: nc.vector.reciprocal(out=smx[0:16, 12:13], in_=smx[0:16, 8:9]), reads=[B_smx], writes=[B_smx])
                tr.op("dve", lambda: nc.vector.tensor_scalar(out=Pf[0:16, 1, :], in0=Pf[0:16, 0, :], scalar1=smx[0:16, 12:13], scalar2=None, op0=ALU.mult),
                      reads=[B_Pf, B_smx], writes=[B_Pf])
                bk, Bbk = nextbank()
                for mc in range(2):
                    tr.op("pe", lambda mc=mc, bk=bk: nc.tensor.transpose(out=bk[:, mc * 16:(mc + 1) * 16], in_=Pf[0:16, 1, mc * 128:(mc + 1) * 128],
                                                                         identity=identf[0:16, 0:16]), reads=[B_Pf, B_const], writes=[Bbk])
                evac(PnT[:], bk[:, 0:32].rearrange("p (mc j) -> p mc j", j=16), [Bbk], [B_PnT])
                bko, Bbko = nextbank()
                for s in range(4):
                    tr.dma("sp", lambda s=s: nc.sync.dma_start(out=Mk, in_=cmv[s].rearrange("(mc p) n -> p mc n", p=128)), writes=[B_Mk])
                    for h in range(4):
                        for mc in range(2):
                            tr.op("pe", lambda s=s, h=h, mc=mc: nc.tensor.matmul(bko[:, h * 4 + s:h * 4 + s + 1], lhsT=Mk[:, mc, h * 128:(h + 1) * 128],
                                                                                 rhs=PnT[:, mc, s * 4 + h:s * 4 + h + 1], start=(mc == 0), stop=(mc == 1)),
                                  reads=[B_Mk, B_PnT], writes=[Bbko])
                evac(yxaT[:, :, 0:NT], bko[:, 0:16].rearrange("p (h s) -> p h s", s=4), [Bbko], [B_yxa])
                ybr = [(ysbT, B_ysbT), (yconvT, B_ycv), (yxaT, B_yxa)]
                for half in range(2):
                    for br in range(3):
                        g3, Bg = next_chunk(4 + half * 3 + br)
                        b3, Bb = next_chunk(10 + half * 3 + br)
                        yb, Byb = ybr[br]
                        for ocl in range(4):
                            oc = half * 4 + ocl
                            bk, Bbk = nextbank()
                            mm8(bk[:, 0:NT], Bbk, lambda c, ocl=ocl: g3[:, c, ocl * 128:(ocl + 1) * 128], lambda c: X0[:, c, 0:NT], [Bg, BX0])
                            tr.op("act", lambda bk=bk, br=br, oc=oc: nc.scalar.activation(out=gsb[:, 0:NT], in_=bk[:, 0:NT], func=AF.Sigmoid,
                                                                                         bias=bgt[:, br * 8 + oc:br * 8 + oc + 1], scale=1.0),
                                  reads=[Bbk, B_const], writes=[B_gsb])
                            bk2, Bbk2 = nextbank()
                            mm8(bk2[:, 0:NT], Bbk2, lambda c, ocl=ocl: b3[:, c, ocl * 128:(ocl + 1) * 128], lambda c, yb=yb: yb[:, c, 0:NT], [Bb, Byb], n=4)
                            if br == 0:
                                tr.op("dve", lambda bk2=bk2, ocl=ocl: nc.vector.tensor_tensor(out=macc[:, ocl, 0:NT], in0=gsb[:, 0:NT], in1=bk2[:, 0:NT], op=ALU.mult),
                                      reads=[Bbk2, B_gsb], writes=[B_macc])
                            else:
                                tr.op("dve", lambda bk2=bk2: nc.vector.tensor_tensor(out=acc[:, 0:NT], in0=gsb[:, 0:NT], in1=bk2[:, 0:NT], op=ALU.mult),
                                      reads=[Bbk2, B_gsb], writes=[B_acc])
                                if br == 1:
                                    tr.op("dve", lambda ocl=ocl: nc.vector.tensor_tensor(out=macc[:, ocl, 0:NT], in0=macc[:, ocl, 0:NT], in1=acc[:, 0:NT], op=ALU.add),
                                          reads=[B_acc, B_macc], writes=[B_macc])
                                else:
                                    tr.op("dve", lambda ocl=ocl, oc=oc: nc.vector.tensor_tensor(out=mT[:, oc, 0:NT], in0=macc[:, ocl, 0:NT], in1=acc[:, 0:NT], op=ALU.add),
                                          reads=[B_acc, B_macc], writes=[B_mT])
                w0, Bw0 = next_chunk(16)
                w1, Bw1 = next_chunk(17)
                pb, Bpb = pairbank()
                for half, (w3, Bw3) in enumerate(((w0, Bw0), (w1, Bw1))):
                    mm8(pb[0:4, half * 512:(half + 1) * 512], Bpb[half], lambda c: mT[:, c, 0:4], lambda c, w3=w3: w3[:, c, :], [B_mT, Bw3])
                i = state["tile"]; state["tile"] += 1
                sc, BS = rstd_of(pb[0:4, :], Bpb, 4, i)
                tr.op("dve", lambda pb=pb, sc=sc: nc.vector.scalar_tensor_tensor(out=xt[0][0:4, :], in0=pb[0:4, :], scalar=sc, in1=gpost_b[0:4, :],
                                                                                 op0=ALU.mult, op1=ALU.mult), reads=Bpb + [BS, B_const], writes=[B_xt[0]])
                tr.op("dve", lambda: nc.vector.tensor_tensor(out=xres[0:4, 0, :], in0=xt[0][0:4, :], in1=xres[0:4, 0, :], op=ALU.add),
                      reads=[B_xt[0], B_xres], writes=[B_xres])
                norm_T(xres[0:4, 0, :], [B_xres], 4, HN, BHN, 0, gfpreT)
                for j in range(8):
                    u3, Bu = next_chunk(18 + j)
                    for ocl in range(4):
                        fc = 4 * j + ocl
                        bk, Bbk = nextbank()
                        mm8(bk[:, 0:NT], Bbk, lambda c, ocl=ocl: u3[:, c, ocl * 128:(ocl + 1) * 128], lambda c: HN[:, c, 0:NT], [Bu, BHN])
                        tr.op("act", lambda bk=bk: nc.scalar.activation(out=sqb[:, 0:NT], in_=bk[:, 0:NT], func=AF.Square), reads=[Bbk], writes=[B_sqb])
                        tr.op("dve", lambda bk=bk, fc=fc: nc.vector.scalar_tensor_tensor(out=f1T[:, fc, 0:NT], in0=bk[:, 0:NT], scalar=0.0, in1=sqb[:, 0:NT],
                                                                                         op0=ALU.is_gt, op1=ALU.mult), reads=[Bbk, B_sqb], writes=[B_f1T])
                pb, Bpb = pairbank()
                for half in range(2):
                    for kg in range(4):
                        d3, Bd = next_chunk(26 + half * 4 + kg)
                        for kc in range(8):
                            tr.op("pe", lambda kc=kc, kg=kg, half=half, d3=d3: nc.tensor.matmul(pb[0:4, half * 512:(half + 1) * 512], lhsT=f1T[:, kg * 8 + kc, 0:4],
                                                                                               rhs=d3[:, kc, :], start=(kg == 0 and kc == 0),
                                                                                               stop=(kg == 3 and kc == 7)),
                                  reads=[B_f1T, Bd], writes=[Bpb[half]])
                i = state["tile"]; state["tile"] += 1
                sc, BS = rstd_of(pb[0:4, :], Bpb, 4, i)
                tr.op("dve", lambda pb=pb, sc=sc: nc.vector.scalar_tensor_tensor(out=xt[0][0:4, :], in0=pb[0:4, :], scalar=sc, in1=gfpost_b[0:4, :],
                                                                                 op0=ALU.mult, op1=ALU.mult), reads=Bpb + [BS, B_const], writes=[B_xt[0]])
                tr.op("dve", lambda: nc.vector.tensor_tensor(out=xt[0][0:4, :], in0=xt[0][0:4, :], in1=xres[0:4, 0, :], op=ALU.add),
                      reads=[B_xt[0], B_xres], writes=[B_xt[0]])
                B_o = Buf("yso"); out_bufs.append(B_o)
                tr.dma("sp", lambda: nc.sync.dma_start(out=ys_o, in_=xt[0][0:4, :]), reads=[B_xt[0]], writes=[B_o], key="yso")
            barrier()
        tr.finish(out_bufs)
    return nc


def _consts():
    p = np.arange(128)
    tri = (p[:, None] >= p[None, :]).astype(np.float32)
    bf = ml_dtypes.bfloat16
    return dict(ident=np.eye(128, dtype=np.float32).astype(bf), tri=tri.astype(bf), fixm=(1 - tri).astype(bf),
                identf=np.eye(128, dtype=np.float32))


def main_inputs(inp, c, G):
    b, i = c // 4, c % 4
    og = own_groups(i, G)
    xp = inp["x_prompt"]
    d = dict(_consts())
    d["xb"] = np.ascontiguousarray(xp[b])
    d["xo"] = np.concatenate([xp[b, g * 512:(g + 1) * 512] for g in og])
    xh = np.zeros((8, D), np.float32)
    for k, g in enumerate(og):
        if g > 0:
            xh[2 * k:2 * k + 2] = xp[b, g * 512 - 2:g * 512]
    d["xh"] = xh
    d["maskwin"] = make_maskwin(i)
    d["mem"] = np.ascontiguousarray(inp["mem_prompt"][b])
    gT = lambda a: np.ascontiguousarray(np.asarray(a).reshape(8, 128).T)
    d["g_mix_preT"] = gT(inp["g_mix_pre"][0]); d["g_ffn_preT"] = gT(inp["g_ffn_pre"][0]); d["g_memT"] = gT(inp["g_mem"][0])
    d["g_mix_post"] = np.asarray(inp["g_mix_post"]).reshape(1, D); d["g_ffn_post"] = np.asarray(inp["g_ffn_post"]).reshape(1, D)
    d["b_sb"] = np.asarray(inp["b_sb"]).reshape(1, NH)
    for n in ["w_in", "w_mem_kv", "w_gate", "w_sb_o", "w_conv_o", "w_xa_o", "w_o", "w_up", "w_down"]:
        d[n] = np.ascontiguousarray(np.asarray(inp[n])[0])
    d["wconvT"] = np.ascontiguousarray(np.asarray(inp["w_conv"])[0].reshape(3, 4, 128).transpose(2, 1, 0))
    d["b_gateT"] = np.ascontiguousarray(np.asarray(inp["b_gate"])[0].reshape(24, 128).T)
    return d


def build_sattn(NPHYS, NSEQ=32):
    nc = bass.Bass("TRN2", target_bir_lowering=False)
    din = lambda n, s, d=F32: nc.dram_tensor(n, list(s), d, kind="ExternalInput").ap()
    dout = lambda n, s, d=F32: nc.dram_tensor(n, list(s), d, kind="ExternalOutput").ap()
    kp = din("kp", [NPHYS, 128 * DH]); vp = din("vp", [NPHYS, 128 * DH])
    ptT = din("ptT", [128, NSEQ], I32)
    xs = din("xs", [NSEQ, D]); g_preT = din("g_mix_preT", [128, 8]); wq = din("wq", [D, DH]); bsb1 = din("bsb1", [1, 1])
    ident_d = din("ident", [128, 128], BF16); identf_d = din("identf", [128, 128], F32); strict_d = din("strictf", [128, 128], F32)
    ysb = dout("ysb", [1, NSEQ * DH])
    with contextlib.ExitStack() as es:
        tr = Tracker(nc, es); cx = Ctx(nc, es); out_bufs = []
        ident = cx.sb([128, 128], BF16); identf = cx.sb([128, 128], F32); strictf = cx.sb([128, 128], F32)
        gpreT = cx.sb([128, 8], F32); bsb = cx.sb([128, 1], F32); pt = cx.sb([128, NSEQ], I32)
        onesf = cx.sb([128, 128], F32)
        B_const = Buf("const")
        ld = lambda o, i: tr.dma("sp", lambda: nc.sync.dma_start(out=o, in_=i), writes=[B_const], key="const")
        ld(ident[:], ident_d); ld(identf[:], identf_d); ld(strictf[:], strict_d); ld(gpreT[:], g_preT)
        ld(bsb[:], bsb1.partition_broadcast(128)); ld(pt[:], ptT)
        tr.op("dve", lambda: nc.vector.memset(onesf[:], 1.0), writes=[B_const])
        bigs = [cx.ps([128, 1024], F32, name="big%d" % i) for i in range(4)]
        banks = [bigs[i // 2][:, (i % 2) * 512:(i % 2 + 1) * 512] for i in range(8)]
        Bbank = [Buf("bank%d" % i) for i in range(8)]
        xt = cx.sb([NSEQ, D], F32); xnb = cx.sb([NSEQ, D], BF16); ssm = cx.sb([NSEQ, 2], F32); B_x = Buf("x")
        xnT = cx.sb([128, 8, NSEQ], BF16); B_xnT = Buf("xnT")
        wqb = cx.sb([128, 8, DH], BF16); B_wq = Buf("wq")
        tr.dma("pool", lambda: nc.gpsimd.dma_start(out=wqb[:], in_=wq.rearrange("(c p) n -> p c n", p=128)), writes=[B_wq])
        tr.dma("sp", lambda: nc.sync.dma_start(out=xt[:], in_=xs), writes=[B_x])
        tr.op("act", lambda: nc.scalar.activation(out=xnb[:], in_=xt[:], func=AF.Square, accum_out=ssm[:, 0:1]), reads=[B_x], writes=[B_x])
        tr.op("act", lambda: nc.scalar.activation(out=ssm[:, 1:2], in_=ssm[:, 0:1], func=AF.Ln, bias=float(EPS), scale=1.0 / D), reads=[B_x], writes=[B_x])
        tr.op("act", lambda: nc.scalar.activation(out=ssm[:, 0:1], in_=ssm[:, 1:2], func=AF.Exp, scale=-0.5), reads=[B_x], writes=[B_x])
        tr.op("dve", lambda: nc.vector.tensor_scalar(out=xnb[:], in0=xt[:], scalar1=ssm[:, 0:1], scalar2=None, op0=ALU.mult), reads=[B_x], writes=[B_x])
        bkb = banks[0].bitcast(BF16)
        for c in range(8):
            tr.op("pe", lambda c=c: nc.tensor.transpose(out=bkb[:, c * 128:c * 128 + NSEQ], in_=xnb[:, c * 128:(c + 1) * 128],
                                                        identity=ident[0:NSEQ, 0:NSEQ]), reads=[B_x, B_const], writes=[Bbank[0]])
        tr.op("dve", lambda: nc.vector.tensor_tensor(out=xnT[:], in0=bkb.rearrange("p (c t) -> p c t", t=128)[:, :, 0:NSEQ],
                                                     in1=gpreT[:, :].unsqueeze(2).to_broadcast([128, 8, NSEQ]), op=ALU.mult),
              reads=[Bbank[0], B_const], writes=[B_xnT])
        for c in range(8):
            tr.op("pe", lambda c=c: nc.tensor.matmul(banks[1][0:NSEQ, 0:DH], lhsT=xnT[:, c, :], rhs=wqb[:, c, :], start=(c == 0), stop=(c == 7)),
                  reads=[B_xnT, B_wq], writes=[Bbank[1]])
        q_sb = cx.sb([NSEQ, DH], F32); Qdiag = cx.sb([NSEQ, NSEQ, DH], F32); qrep = cx.sb([128, NSEQ, DH], F32); B_q = Buf("q")
        tr.op("dve", lambda: nc.vector.tensor_copy(out=q_sb[:], in_=banks[1][0:NSEQ, 0:DH]), reads=[Bbank[1]], writes=[B_q])
        tr.op("dve", lambda: nc.vector.tensor_tensor(out=Qdiag[:], in0=q_sb[:].unsqueeze(1).to_broadcast([NSEQ, NSEQ, DH]),
                                                     in1=identf[0:NSEQ, 0:NSEQ].unsqueeze(2).to_broadcast([NSEQ, NSEQ, DH]), op=ALU.mult),
              reads=[B_q, B_const], writes=[B_q])
        nq = NSEQ * DH // 512
        for j in range(nq):
            tr.op("pe", lambda j=j: nc.tensor.matmul(banks[2 + j % 4], lhsT=onesf[0:NSEQ, :],
                                                     rhs=Qdiag[:].rearrange("a s d -> a (s d)")[:, j * 512:(j + 1) * 512], start=True, stop=True),
                  reads=[B_q, B_const], writes=[Bbank[2 + j % 4]])
            tr.op("dve", lambda j=j: nc.vector.tensor_copy(out=qrep[:].rearrange("p s d -> p (s d)")[:, j * 512:(j + 1) * 512], in_=banks[2 + j % 4]),
                  reads=[Bbank[2 + j % 4]], writes=[B_q])
        Kt = [cx.sb([128, 128, DH], F32) for _ in range(2)]; B_K = [Buf("K0"), Buf("K1")]
        Vt = [cx.sb([128, 128 * DH], F32) for _ in range(2)]; B_V = [Buf("V0"), Buf("V1")]
        Vb = [cx.sb([128, 128 * DH], BF16) for _ in range(2)]; B_Vb = [Buf("Vb0"), Buf("Vb1")]
        Z = cx.sb([128, 128], F32); Pre = cx.sb([128, 128], F32); Ee = cx.sb([128, 128], F32); Ll = cx.sb([128, 128], F32)
        Wt = [cx.sb([128, 128], BF16) for _ in range(2)]; B_W = [Buf("W0"), Buf("W1")]
        Asm = cx.sb([128, 2], F32)
        B_Z = Buf("Z"); B_E = Buf("E"); B_L = Buf("L"); B_P = Buf("Pre"); B_A = Buf("A")
        ybank = bigs[3]; By = [Bbank[6], Bbank[7]]
        ybank2 = bigs[2]; By2 = [Bbank[4], Bbank[5]]

        def gather(s):
            i = s % 2
            tr.dma("pool", lambda: nc.gpsimd.indirect_dma_start(out=Kt[i][:].rearrange("p t d -> p (t d)"), out_offset=None, in_=kp,
                                                                in_offset=bass.IndirectOffsetOnAxis(ap=pt[:, s:s + 1], axis=0)),
                   reads=[B_const], writes=[B_K[i]])
            tr.dma("pool", lambda: nc.gpsimd.indirect_dma_start(out=Vt[i][:], out_offset=None, in_=vp,
                                                                in_offset=bass.IndirectOffsetOnAxis(ap=pt[:, s:s + 1], axis=0)),
                   reads=[B_const], writes=[B_V[i]])
        gather(0)
        for s in range(NSEQ):
            i = s % 2
            if s + 1 < NSEQ:
                gather(s + 1)
            tr.op("dve", lambda: nc.vector.tensor_tensor(out=Kt[i][:], in0=Kt[i][:], in1=qrep[:, s:s + 1, :].to_broadcast([128, 128, DH]), op=ALU.mult),
                  reads=[B_K[i], B_q], writes=[B_K[i]])
            tr.op("dve", lambda: nc.vector.tensor_reduce(out=Z[:], in_=Kt[i][:], axis=AX.X, op=ALU.add), reads=[B_K[i]], writes=[B_Z])
            tr.op("act", lambda: nc.scalar.copy(out=Vb[i][:], in_=Vt[i][:]), reads=[B_V[i]], writes=[B_Vb[i]])
            tr.op("dve", lambda: nc.vector.tensor_scalar(out=Z[:], in0=Z[:], scalar1=0.125, scalar2=bsb[:, 0:1], op0=ALU.mult, op1=ALU.add),
                  reads=[B_Z, B_const], writes=[B_Z])
            tr.op("act", lambda: nc.scalar.activation(out=Ee[:], in_=Z[:], func=AF.Exp), reads=[B_Z], writes=[B_E])
            tr.op("act", lambda: nc.scalar.activation(out=Ll[:], in_=Ee[:], func=AF.Ln, bias=1.0, scale=1.0), reads=[B_E], writes=[B_L])
            tr.op("dve", lambda: nc.vector.tensor_tensor_scan(out=Pre[:], data0=onesf[:, :], data1=Ll[:], initial=0.0, op0=ALU.mult, op1=ALU.add),
                  reads=[B_L, B_const], writes=[B_P])
            tr.op("pe", lambda: nc.tensor.matmul(banks[0][:, 0:2], lhsT=strictf[:, :], rhs=Pre[:, 126:128], start=True, stop=True),
                  reads=[B_P, B_const], writes=[Bbank[0]])
            tr.op("dve", lambda: nc.vector.tensor_tensor(out=Asm[:, 0:1], in0=banks[0][:, 1:2], in1=Pre[:, 127:128], op=ALU.add),
                  reads=[Bbank[0], B_P], writes=[B_A])
            tr.op("dve", lambda: nc.vector.tensor_scalar(out=Asm[:, 1:2], in0=Asm[:, 0:1], scalar1=-1.0, scalar2=None, op0=ALU.mult),
                  reads=[B_A], writes=[B_A])
            tr.op("dve", lambda: nc.vector.tensor_tensor(out=Z[:], in0=Z[:], in1=Pre[:], op=ALU.add), reads=[B_Z, B_P], writes=[B_Z])
            tr.op("dve", lambda: nc.vector.tensor_tensor(out=Z[:], in0=Z[:], in1=Ll[:], op=ALU.subtract), reads=[B_Z, B_L], writes=[B_Z])
            tr.op("act", lambda: nc.scalar.activation(out=Wt[i][:], in_=Z[:], func=AF.Exp, bias=Asm[:, 1:2], scale=1.0),
                  reads=[B_Z, B_A], writes=[B_W[i]])
            yb, Byb = (ybank, By) if s < 16 else (ybank2, By2)
            col = (s % 16) * DH
            for t in range(128):
                tr.op("pe", lambda t=t: nc.tensor.matmul(yb[0:1, col:col + DH], lhsT=Wt[i][:, t:t + 1], rhs=Vb[i][:, t * DH:(t + 1) * DH],
                                                         start=(t == 0), stop=(t == 127)), reads=[B_W[i], B_Vb[i]], writes=[Byb[col // 512]])
        yrow = cx.sb([1, NSEQ * DH], F32); B_yr = Buf("yrow")
        n1 = min(NSEQ, 16) * DH
        tr.op("dve", lambda: nc.vector.tensor_copy(out=yrow[:, 0:n1], in_=ybank[0:1, 0:n1]), reads=By, writes=[B_yr])
        if NSEQ > 16:
            tr.op("dve", lambda: nc.vector.tensor_copy(out=yrow[:, n1:], in_=ybank2[0:1, 0:NSEQ * DH - n1]), reads=By2, writes=[B_yr])
        B_o = Buf("o"); out_bufs.append(B_o)
        tr.dma("sp", lambda: nc.sync.dma_start(out=ysb, in_=yrow[:]), reads=[B_yr], writes=[B_o])
        tr.finish(out_bufs)
    return nc


def sattn_inputs(inp, c, pt):
    bf = ml_dtypes.bfloat16
    p = np.arange(128)
    d = dict(ident=np.eye(128, dtype=np.float32).astype(bf), identf=np.eye(128, dtype=np.float32),
             strictf=(p[:, None] > p[None, :]).astype(np.float32))
    ck = np.asarray(inp["cache_k_pages"]); cv_ = np.asarray(inp["cache_v_pages"])
    nph = ck.shape[1]
    d["kp"] = np.ascontiguousarray(ck[0, :, :, c, :]).reshape(nph, 128 * DH)
    d["vp"] = np.ascontiguousarray(cv_[0, :, :, c, :]).reshape(nph, 128 * DH)
    d["ptT"] = np.ascontiguousarray(pt.T.astype(np.int32))
    d["xs"] = np.ascontiguousarray(np.asarray(inp["x_sample"])[:, 0, :])
    d["g_mix_preT"] = np.ascontiguousarray(np.asarray(inp["g_mix_pre"])[0].reshape(8, 128).T)
    d["wq"] = np.ascontiguousarray(np.asarray(inp["w_in"])[0][:, c * DH:(c + 1) * DH])
    d["bsb1"] = np.asarray(inp["b_sb"])[0, c:c + 1].reshape(1, 1).astype(np.float32)
    return d


def sample_inputs(inp, c, ysb_s):
    d = {}
    d["xs4"] = np.ascontiguousarray(np.asarray(inp["x_sample"])[4 * c:4 * c + 4, 0, :])
    y4 = ysb_s[4 * c:4 * c + 4]
    d["ysbT_s"] = np.ascontiguousarray(y4.reshape(4, 4, 128).transpose(2, 1, 0)).astype(np.float32)
    st = np.asarray(inp["state_conv"])[0, 4 * c:4 * c + 4]
    d["stT"] = np.ascontiguousarray(st.reshape(4, 2, 4, 128).transpose(3, 2, 1, 0))
    d["cmk"] = np.ascontiguousarray(np.asarray(inp["cache_mem_k"])[0, 4 * c:4 * c + 4].reshape(4, NMEM, 512))
    d["cmv"] = np.ascontiguousarray(np.asarray(inp["cache_mem_v"])[0, 4 * c:4 * c + 4].reshape(4, NMEM, 512))
    return d


_NC_CACHE = {}


def kernel(**inp):
    inp = {k: np.asarray(v) for k, v in inp.items()}
    SEQ = inp["x_prompt"].shape[1]
    G = SEQ // 512
    NS = G // 4
    nph = inp["cache_k_pages"].shape[1]
    nseq = inp["x_sample"].shape[0]
    pt = inp["page_table"]
    if ("A", nph, nseq) not in _NC_CACHE:
        _NC_CACHE[("A", nph, nseq)] = build_sattn(nph, nseq)
    ncA = _NC_CACHE[("A", nph, nseq)]
    resA = run_bass_kernel_spmd(ncA, [sattn_inputs(inp, c, pt) for c in range(8)], core_ids=list(range(8)))
    ysb_s = np.concatenate([np.asarray(resA.results[c]["ysb"]).reshape(nseq, DH) for c in range(8)], axis=1)
    if ("B", G) not in _NC_CACHE:
        _NC_CACHE[("B", G)] = build_main(G)
    ncB = _NC_CACHE[("B", G)]
    ins = []
    for c in range(8):
        d = main_inputs(inp, c, G)
        d.update(sample_inputs(inp, c, ysb_s))
        ins.append(d)
    res = run_bass_kernel_spmd(ncB, ins, core_ids=list(range(8))).results
    f32 = np.float32
    y_p = np.zeros((2, SEQ, D), f32); k_p = np.zeros((1, 2, SEQ, NH, DH), f32); v_p = np.zeros((1, 2, SEQ, NH, DH), f32)
    conv_p = np.zeros((1, 2, 2, CW), f32); mk_p = np.zeros((1, 2, NMEM, XAH, XAD), f32); mv_p = np.zeros((1, 2, NMEM, XAH, XAD), f32)
    y_s = np.zeros((nseq, 1, D), f32); k_s = np.zeros((1, nseq, 1, NH, DH), f32); v_s = np.zeros((1, nseq, 1, NH, DH), f32)
    conv_s = np.zeros((1, nseq, 2, CW), f32)
    for c in range(8):
        b, i = c // 4, c % 4
        r = res[c]
        for k, g in enumerate(own_groups(i, G)):
            sl = slice(g * 512, (g + 1) * 512)
            y_p[b, sl] = r["y_o"][k * 512:(k + 1) * 512]
            k_p[0, b, sl] = np.asarray(r["k_o"][k * 512:(k + 1) * 512]).reshape(512, NH, DH)
            v_p[0, b, sl] = np.asarray(r["v_o"][k * 512:(k + 1) * 512]).reshape(512, NH, DH)
            if g == G - 1:
                conv_p[0, b] = np.asarray(r["conv_o"])[k].transpose(2, 1, 0).reshape(2, CW)
        if i == 0:
            mk_p[0, b] = np.asarray(r["mk_o"]).reshape(NMEM, XAH, XAD)
            mv_p[0, b] = np.asarray(r["mv_o"]).reshape(NMEM, XAH, XAD)
        y_s[4 * c:4 * c + 4, 0] = r["ys_o"]
        k_s[0, 4 * c:4 * c + 4, 0] = np.asarray(r["ks_o"]).reshape(4, NH, DH)
        v_s[0, 4 * c:4 * c + 4, 0] = np.asarray(r["vs_o"]).reshape(4, NH, DH)
        conv_s[0, 4 * c:4 * c + 4] = np.asarray(r["cs_o"]).transpose(3, 2, 1, 0).reshape(4, 2, CW)
    return (y_p, y_s, k_p, v_p, conv_p, mk_p, mv_p, k_s, v_s, conv_s)
```
